# Optimizing a Trainium2 kernel written in Bass

```python
import jax
import jax.numpy as jnp
from jax import lax
import numpy as np

D_MODEL = 2048
BATCH = 4
SEQ = 8192
DEPTH = 4

N_MIXERS = 3
N_HEADS = 16
HEAD_DIM = D_MODEL // N_HEADS
D_FF = 4 * D_MODEL
CONV_WIDTH = 31
MOBA_BLOCK = 256
MOBA_TOPK = 3
MOBA_Q_CHUNK = 16
FOX_Q_BLOCK = 128
NORM_EPS = 1e-6
NEG_INF = -1e30
LAYER_MIXERS = tuple(i % N_MIXERS for i in range(DEPTH))
N_CONV = LAYER_MIXERS.count(0)
N_MOBA = LAYER_MIXERS.count(1)
N_FOX = LAYER_MIXERS.count(2)

kernel_name = 'hybrid_conv_moba_fox_decoder'


def _rms_norm(x, g):
    xf = x.astype(jnp.float32)
    y = xf * lax.rsqrt(jnp.mean(xf * xf, axis=-1, keepdims=True) + NORM_EPS)
    return (y * g.astype(jnp.float32)).astype(x.dtype)


def _layer_norm(x, g, b):
    xf = x.astype(jnp.float32)
    mu = jnp.mean(xf, axis=-1, keepdims=True)
    var = jnp.mean(jnp.square(xf - mu), axis=-1, keepdims=True)
    y = (xf - mu) * lax.rsqrt(var + NORM_EPS)
    return (y * g.astype(jnp.float32) + b.astype(jnp.float32)).astype(x.dtype)


def _split_heads(z):
    b, s, _ = z.shape
    return z.reshape(b, s, N_HEADS, HEAD_DIM).transpose(0, 2, 1, 3)


def _merge_blocks(o):
    nb, b, h, q, d = o.shape
    return o.transpose(1, 0, 3, 2, 4).reshape(b, nb * q, h * d)


def _conv_mixer(h, w_in, b_in, dw_w, dw_b, ln_g, ln_b, w_out, b_out):
    u = h @ w_in + b_in
    a, gt = jnp.split(u, 2, axis=-1)
    u = a * jax.nn.sigmoid(gt)
    u = lax.conv_general_dilated(
        u, dw_w[:, None, :].astype(u.dtype), window_strides=(1,),
        padding=[(CONV_WIDTH - 1, 0)],
        dimension_numbers=('NWC', 'WIO', 'NWC'),
        feature_group_count=D_MODEL) + dw_b
    u = jax.nn.silu(_layer_norm(u, ln_g, ln_b))
    return u @ w_out + b_out


def _moba_mixer(h, w_qkv, q_g, k_g, w_o):
    b, s, _ = h.shape
    q, k, v = jnp.split(h @ w_qkv, 3, axis=-1)
    q = _rms_norm(_split_heads(q), q_g)
    k = _rms_norm(_split_heads(k), k_g)
    v = _split_heads(v)
    nb = -(-s // MOBA_BLOCK)
    pad = nb * MOBA_BLOCK - s
    kp = jnp.pad(k, ((0, 0), (0, 0), (0, pad), (0, 0)))
    vp = jnp.pad(v, ((0, 0), (0, 0), (0, pad), (0, 0)))
    kb = kp.reshape(b, N_HEADS, nb, MOBA_BLOCK, HEAD_DIM)
    vb = vp.reshape(b, N_HEADS, nb, MOBA_BLOCK, HEAD_DIM)
    kmean = jnp.mean(kb, axis=3)
    topk = min(MOBA_TOPK, nb)
    scale = HEAD_DIM ** -0.5
    n_chunks = s // MOBA_Q_CHUNK
    qc = q.reshape(b, N_HEADS, n_chunks, MOBA_Q_CHUNK, HEAD_DIM).transpose(2, 0, 1, 3, 4)
    blk_ids = jnp.arange(nb)
    gather = jax.vmap(jax.vmap(lambda kk, ii: kk[ii]))

    def chunk(args):
        ci, qi = args
        t0 = ci * MOBA_Q_CHUNK
        cur = t0 // MOBA_BLOCK
        gate = jnp.einsum('bhqd,bhnd->bhqn', qi, kmean).astype(jnp.float32)
        gate = jnp.where(blk_ids < cur, gate, NEG_INF)
        _, sel = lax.top_k(gate, topk)
        valid = sel < cur
        k_sel = gather(kb, sel)
        v_sel = gather(vb, sel)
        s_sel = jnp.einsum('bhqd,bhqkjd->bhqkj', qi, k_sel).astype(jnp.float32) * scale
        s_sel = jnp.where(valid[..., None], s_sel, NEG_INF)
        s_sel = s_sel.reshape(b, N_HEADS, MOBA_Q_CHUNK, topk * MOBA_BLOCK)
        k_own = lax.dynamic_slice_in_dim(kp, cur * MOBA_BLOCK, MOBA_BLOCK, axis=2)
        v_own = lax.dynamic_slice_in_dim(vp, cur * MOBA_BLOCK, MOBA_BLOCK, axis=2)
        s_own = jnp.einsum('bhqd,bhjd->bhqj', qi, k_own).astype(jnp.float32) * scale
        t_pos = t0 + jnp.arange(MOBA_Q_CHUNK)
        s_pos = cur * MOBA_BLOCK + jnp.arange(MOBA_BLOCK)
        s_own = jnp.where(s_pos[None, :] <= t_pos[:, None], s_own, NEG_INF)
        p = jax.nn.softmax(jnp.concatenate([s_sel, s_own], axis=-1), axis=-1)
        p_sel = p[..., :topk * MOBA_BLOCK].reshape(b, N_HEADS, MOBA_Q_CHUNK, topk, MOBA_BLOCK)
        p_own = p[..., topk * MOBA_BLOCK:]
        o = jnp.einsum('bhqkj,bhqkjd->bhqd', p_sel.astype(v.dtype), v_sel)
        o = o + jnp.einsum('bhqj,bhjd->bhqd', p_own.astype(v.dtype), v_own)
        return o

    o = lax.map(chunk, (jnp.arange(n_chunks), qc))
    return _merge_blocks(o) @ w_o


def _fox_mixer(h, w_in, b_f, q_g, k_g, w_o):
    b, s, _ = h.shape
    proj = h @ w_in
    d = D_MODEL
    q = _rms_norm(_split_heads(proj[..., :d]), q_g)
    k = _rms_norm(_split_heads(proj[..., d:2 * d]), k_g)
    v = _split_heads(proj[..., 2 * d:3 * d])
    g_out = proj[..., 3 * d:4 * d]
    log_f = jax.nn.log_sigmoid((proj[..., 4 * d:] + b_f).astype(jnp.float32))
    cum = jnp.cumsum(log_f, axis=1).transpose(0, 2, 1)
    scale = HEAD_DIM ** -0.5
    nqb = s // FOX_Q_BLOCK
    qb = q.reshape(b, N_HEADS, nqb, FOX_Q_BLOCK, HEAD_DIM).transpose(2, 0, 1, 3, 4)
    cq = cum.reshape(b, N_HEADS, nqb, FOX_Q_BLOCK).transpose(2, 0, 1, 3)
    s_pos = jnp.arange(s)

    def block(args):
        bi, qi, ci = args
        t_pos = bi * FOX_Q_BLOCK + jnp.arange(FOX_Q_BLOCK)
        logits = jnp.einsum('bhqd,bhsd->bhqs', qi, k).astype(jnp.float32) * scale
        logits = logits + (ci[..., :, None] - cum[:, :, None, :])
        logits = jnp.where(s_pos[None, :] <= t_pos[:, None], logits, NEG_INF)
        p = jax.nn.softmax(logits, axis=-1)
        return jnp.einsum('bhqs,bhsd->bhqd', p.astype(v.dtype), v)

    o = _merge_blocks(lax.map(block, (jnp.arange(nqb), qb, cq)))
    o = o * jax.nn.sigmoid(g_out)
    return o @ w_o


def _normal(k, shape, scale):
    return scale * jax.random.normal(k, shape, jnp.float32)


def setup_inputs(seed: int = 0) -> dict:
    key = jax.random.key(seed)
    ks = iter(jax.random.split(key, 32))
    d = D_MODEL
    sd = d ** -0.5
    return {
        'x': _normal(next(ks), (BATCH, SEQ, d), 1.0),
        'c': _normal(next(ks), (BATCH, d), 1.0),
        'norm1_g': 1.0 + _normal(next(ks), (DEPTH, d), 0.02),
        'norm2_g': 1.0 + _normal(next(ks), (DEPTH, d), 0.02),
        'mod_w': _normal(next(ks), (DEPTH, d, 6 * d), 0.5 * sd),
        'mod_b': _normal(next(ks), (DEPTH, 6 * d), 0.02),
        'mlp_w1': _normal(next(ks), (DEPTH, d, D_FF), sd),
        'mlp_w2': _normal(next(ks), (DEPTH, D_FF, d), D_FF ** -0.5),
        'conv_w_in': _normal(next(ks), (N_CONV, d, 2 * d), sd),
        'conv_b_in': _normal(next(ks), (N_CONV, 2 * d), 0.02),
        'conv_dw_w': _normal(next(ks), (N_CONV, CONV_WIDTH, d), CONV_WIDTH ** -0.5),
        'conv_dw_b': _normal(next(ks), (N_CONV, d), 0.02),
        'conv_ln_g': 1.0 + _normal(next(ks), (N_CONV, d), 0.02),
        'conv_ln_b': _normal(next(ks), (N_CONV, d), 0.02),
        'conv_w_out': _normal(next(ks), (N_CONV, d, d), sd),
        'conv_b_out': _normal(next(ks), (N_CONV, d), 0.02),
        'moba_w_qkv': _normal(next(ks), (N_MOBA, d, 3 * d), sd),
        'moba_q_g': 1.0 + _normal(next(ks), (N_MOBA, HEAD_DIM), 0.02),
        'moba_k_g': 1.0 + _normal(next(ks), (N_MOBA, HEAD_DIM), 0.02),
        'moba_w_o': _normal(next(ks), (N_MOBA, d, d), sd),
        'fox_w_in': _normal(next(ks), (N_FOX, d, 4 * d + N_HEADS), sd),
        'fox_b_f': 2.0 + _normal(next(ks), (N_FOX, N_HEADS), 0.5),
        'fox_q_g': 1.0 + _normal(next(ks), (N_FOX, HEAD_DIM), 0.02),
        'fox_k_g': 1.0 + _normal(next(ks), (N_FOX, HEAD_DIM), 0.02),
        'fox_w_o': _normal(next(ks), (N_FOX, d, d), sd),
    }


def reference(x, c, norm1_g, norm2_g, mod_w, mod_b, mlp_w1, mlp_w2,
              conv_w_in, conv_b_in, conv_dw_w, conv_dw_b, conv_ln_g, conv_ln_b,
              conv_w_out, conv_b_out, moba_w_qkv, moba_q_g, moba_k_g, moba_w_o,
              fox_w_in, fox_b_f, fox_q_g, fox_k_g, fox_w_o):
    cond = jax.nn.silu(c)
    for i in range(DEPTH):
        kind = LAYER_MIXERS[i]
        j = i // N_MIXERS
        mod = cond @ mod_w[i] + mod_b[i]
        sh1, sc1, g1, sh2, sc2, g2 = jnp.split(mod, 6, axis=-1)
        h = _rms_norm(x, norm1_g[i]) * (1.0 + sc1[:, None, :]) + sh1[:, None, :]
        if kind == 0:
            y = _conv_mixer(h, conv_w_in[j], conv_b_in[j], conv_dw_w[j], conv_dw_b[j],
                            conv_ln_g[j], conv_ln_b[j], conv_w_out[j], conv_b_out[j])
        elif kind == 1:
            y = _moba_mixer(h, moba_w_qkv[j], moba_q_g[j], moba_k_g[j], moba_w_o[j])
        else:
            y = _fox_mixer(h, fox_w_in[j], fox_b_f[j], fox_q_g[j], fox_k_g[j], fox_w_o[j])
        x = x + (1.0 + g1[:, None, :]) * y
        h = _rms_norm(x, norm2_g[i]) * (1.0 + sc2[:, None, :]) + sh2[:, None, :]
        u = jnp.square(jax.nn.relu(h @ mlp_w1[i]))
        x = x + (1.0 + g2[:, None, :]) * (u @ mlp_w2[i])
    return x
```

```python
import numpy as np
from contextlib import ExitStack
import concourse.bass as bass
import concourse.mybir as mybir
from concourse.bass_utils import run_bass_kernel_spmd

F32 = mybir.dt.float32
BF16 = mybir.dt.bfloat16
AF = mybir.ActivationFunctionType
ALU = mybir.AluOpType
AX = mybir.AxisListType

P = 128
D = 2048
DC = 16
FF = 8192
FC = 64
SEQ = 8192
TOK = 4096
T = 512
NT = 8
HALO = 32
NH = 8
EPS = 1e-6
CW = 31
NEGM = -30000.0
PAIRS = [[0, 1], [2, 3], [4, 5], [6, 7]]
LAYER_KIND = (0, 1, 2, 0)

ENG_ATTR = {'pe': 'tensor', 'act': 'scalar', 'dve': 'vector', 'pool': 'gpsimd', 'sp': 'sync'}
COMPUTE = ('pe', 'act', 'dve', 'pool')


class Res:
    __slots__ = ('name', 'w', 'wj', 'r', 'excl')

    def __init__(self, name='', excl=False):
        self.name = name
        self.w = {}
        self.wj = {}
        self.r = {}
        self.excl = excl


def PRes(name=''):
    return Res(name, excl=True)


class Sched:
    POOLS = {'dma': (28, 16), 'bg': (16, 16), 'cc': (6, 1)}

    def __init__(self, nc, stack):
        self.nc = nc
        self.ops = {e: [] for e in ENG_ATTR}
        self.cnt = {e: 0 for e in COMPUTE}
        self.known = {e: {} for e in ENG_ATTR}
        self.tot = {}
        self.rot = {p: 0 for p in self.POOLS}
        self.sems = {}
        for e in COMPUTE:
            self.sems[e] = stack.enter_context(nc.semaphore('prog_' + e))
        for p, (n, _) in self.POOLS.items():
            for i in range(n):
                self.sems[(p, i)] = stack.enter_context(nc.semaphore('%s%d' % (p, i)))
                self.tot[(p, i)] = 0
        self.nblocks = 0

    def _collect(self, eng, reads, writes, joins, extra=None):
        need = {}

        def add(k, v):
            if need.get(k, 0) < v:
                need[k] = v
        for r in reads:
            for k, v in r.w.items():
                add(k, v)
            for k, v in r.wj.items():
                add(k, v)
            if r.excl:
                for k, v in r.r.items():
                    if k != eng:
                        add(k, v)
        for w in writes:
            for k, v in w.w.items():
                add(k, v)
            for k, v in w.wj.items():
                add(k, v)
            for k, v in w.r.items():
                add(k, v)
        for w in joins:
            for k, v in w.w.items():
                add(k, v)
            for k, v in w.r.items():
                add(k, v)
        if extra:
            add(*extra)
        known = self.known[eng]
        waits = []
        for k, v in need.items():
            if known.get(k, 0) < v:
                known[k] = v
                waits.append((k, v))
        return waits

    def _commit(self, ev, reads, writes, joins):
        k, v = ev
        for r in reads:
            if r.r.get(k, 0) < v:
                r.r[k] = v
        for w in writes:
            w.w = {k: v}
            w.wj = {}
            w.r = {}
        for w in joins:
            if w.wj.get(k, 0) < v:
                w.wj[k] = v

    def op(self, eng, fn, reads=(), writes=(), joins=()):
        waits = self._collect(eng, reads, writes, joins)
        self.cnt[eng] += 1
        ev = (eng, self.cnt[eng])
        self.ops[eng].append((waits, fn, ev, 1))
        self._commit(ev, reads, writes, joins)

    def aop(self, eng, fn, pool='dma', reads=(), writes=(), joins=()):
        n, inc = self.POOLS[pool]
        s = self.rot[pool]
        self.rot[pool] = (s + 1) % n
        key = (pool, s)
        prev = (key, self.tot[key]) if self.tot[key] else None
        waits = self._collect(eng, reads, writes, joins, extra=prev)
        self.tot[key] += inc
        ev = (key, self.tot[key])
        self.ops[eng].append((waits, fn, ev, inc))
        self._commit(ev, reads, writes, joins)

    def flush(self, name, drain=('dma',)):
        nc = self.nc
        final = []
        for key, t in self.tot.items():
            if t and key[0] in drain and self.known['sp'].get(key, 0) < t:
                self.known['sp'][key] = t
                final.append((key, t))
        sems = self.sems
        oplists = self.ops
        self.ops = {e: [] for e in ENG_ATTR}
        self.nblocks += 1
        with nc.Block() as block:
            def make(ename):
                oplist = oplists[ename]

                def body(engine):
                    for waits, fn, ev, inc in oplist:
                        for k, v in waits:
                            engine.wait_ge(sems[k], v)
                        ins = fn(engine)
                        ins.then_inc(sems[ev[0]], inc)
                    if ename == 'sp':
                        for k, v in final:
                            engine.wait_ge(sems[k], v)
                return body
            for ename, attr in ENG_ATTR.items():
                getattr(block, attr)(make(ename))


def vec_layout():
    off = {}
    n = 0

    def add(name, w):
        nonlocal n
        off[name] = n
        n += w
    add('c', 16)
    for l in range(4):
        add('n1g%d' % l, 16)
        add('n2g%d' % l, 16)
    add('modb', 192)
    for cj in range(2):
        add('cbia%d' % cj, 16)
        add('cbig%d' % cj, 16)
        add('cdww%d' % cj, 16 * CW)
        add('cdwb%d' % cj, 16)
        add('clng%d' % cj, 16)
        add('clnb%d' % cj, 16)
        add('cbo%d' % cj, 16)
    for nm in ('mqg', 'mkg', 'fqg', 'fkg', 'fbf', 'flag'):
        add(nm, 1)
    return off, n


VOFF, NV = vec_layout()
CST_IDENT, CST_TRI, CST_ONES, CST_E, CST_EF = 0, 128, 256, 384, 384 + 32 * 128
NCST = CST_EF + 8 * 128

FULLW = {}
for _l in range(4):
    FULLW['w1_%d' % _l] = (D, FF, 'mlp_w1h_%d' % _l, None)
    FULLW['w2_%d' % _l] = (FF, D, 'mlp_w2h_%d' % _l, None)
for _c in range(2):
    FULLW['cin_%d' % _c] = (D, 2 * D, 'conv_w_in_h_%d' % _c, None)
    FULLW['cout_%d' % _c] = (D, D, 'conv_w_out_h_%d' % _c, None)
FULLW['mwo'] = (D, D, 'moba_wo_h', None)
FULLW['fwo'] = (D, D, 'fox_wo_h', None)
BUNDLES = [['cin_0', 'cout_0'], ['w1_0'], ['w2_0'], ['mwo'], ['w1_1'], ['w2_1'], ['fwo'],
           ['w1_2'], ['w2_2'], ['cin_1', 'cout_1'], ['w1_3'], ['w2_3']]


def wgeom(K, N):
    nw = 512 if K == D else 128
    return nw, N // nw, (K // P) // 2


class Builder:
    def __init__(self, stop_after=99, pairs=None, dumps=(), opts=()):
        self.opts = set(opts)
        self.stop_after = stop_after
        self.pairs = pairs or PAIRS
        self.dumps = []
        self.dump_names = dumps
        self.nc = bass.Bass("TRN2", target_bir_lowering=False)
        self.inputs = {}
        self.st = ExitStack()
        self.S = None

    def inp(self, name, shape, dtype=F32):
        if name not in self.inputs:
            t = self.nc.dram_tensor(name, list(shape), dtype, kind="ExternalInput")
            self.inputs[name] = (t, tuple(shape))
        return self.inputs[name][0]

    def dram(self, name, shape, dtype):
        return self.nc.dram_tensor(name, list(shape), dtype)

    def _uid(self, name):
        self._n = getattr(self, '_n', 0) + 1
        return '%s_%d' % (name, self._n)

    def sb(self, stack, name, shape, dtype):
        return stack.enter_context(self.nc.sbuf_tensor(self._uid(name), list(shape), dtype))

    def ps(self, stack, name):
        return stack.enter_context(self.nc.psum_tensor(self._uid(name), [P, 512], F32))

    def setup_weights(self):
        self.wch = {}
        self.wdone = set()
        for nm, (K, N, _, _) in FULLW.items():
            nw, ng, kh = wgeom(K, N)
            lst = []
            for j in range(ng // 2):
                lst.append(dict(half=self.dram('wh_%s_%d' % (nm, j), [256, 4096], BF16),
                                full=self.dram('wf_%s_%d' % (nm, j), [512, 4096], BF16),
                                Rh=Res(), Rf=Res()))
            self.wch[nm] = lst
        self.lrows = (6 + 8) * P
        self.lfull = self.dram('wlf', [2 * self.lrows, 4096], BF16)
        self.Rl = Res('wlf')
        self.lloc = {'mqkv': 0, 'fw4': 6 * P}

    def cast_group(self, dst2d, row0, kh, nw, src_ap_rows, segs, join):
        S = self.S
        dst = dst2d[row0:row0 + P, :].rearrange("p (kk n) -> p kk n", n=nw)
        o = 0
        for (c0, w) in segs:
            src = src_ap_rows[:, c0:c0 + w].rearrange("(kk p) n -> p kk n", p=P)
            d = dst[:, :, o:o + w]
            S.aop('pool', lambda e, d=d, src=src: e.dma_start(out=d, in_=src), pool='bg', joins=[join])
            o += w

    def allgather(self, src_t, dst_t, Rsrc, Rdst):
        self.S.aop('pool', lambda e: e.collective_compute("AllGather", ALU.bypass, replica_groups=self.pairs,
                                                          ins=[src_t.ap().opt()], outs=[dst_t.ap().opt()]),
                   pool='cc', reads=[Rsrc], writes=[Rdst])

    def prep_bundle(self, bi):
        for nm in BUNDLES[bi]:
            if nm in self.wdone:
                continue
            self.wdone.add(nm)
            K, N, iname, idx = FULLW[nm]
            nw, ng, kh = wgeom(K, N)
            t = self.inp(iname, [K // 2, N])
            src = t.ap()
            for g in range(ng):
                ch = self.wch[nm][g // 2]
                if nm.startswith('cin'):
                    segs = [(g * 256, 256), (D + g * 256, 256)]
                else:
                    segs = [(g * nw, nw)]
                self.cast_group(ch['half'].ap(), (g % 2) * P, kh, nw, src, segs, ch['Rh'])
                if g % 2 == 1:
                    self.allgather(ch['half'], ch['full'], ch['Rh'], ch['Rf'])

    def prep_local(self):
        S = self.S
        lf = self.lfull.ap()
        for nm, iname, ncols, r0 in (('mqkv', 'moba_qkv_l', 3072, 0), ('fw4', 'fox_w4_l', 4096, 6 * P)):
            t = self.inp(iname, [D, ncols])
            for khalf in range(2):
                src = t.ap()[khalf * 1024:(khalf + 1) * 1024, :]
                for g in range(ncols // 512):
                    self.cast_group(lf, khalf * self.lrows + r0 + g * P, 8, 512, src, [(g * 512, 512)], self.Rl)

    def slab_src(self, wname, g):
        if wname in self.lloc:
            r0 = self.lloc[wname] + g * P
            v = self.lfull.ap().rearrange("(k r) c -> r k c", k=2)
            return v[r0:r0 + P], self.Rl
        ch = self.wch[wname][g // 2]
        v = ch['full'].ap().rearrange("(k r) c -> r k c", k=2)
        return v[(g % 2) * P:(g % 2 + 1) * P], ch['Rf']

    def load_slab(self, wname, g):
        S = self.S
        i = self.slab_i
        self.slab_i = (i + 1) % len(self.slabs)
        buf, R = self.slabs[i], self.Rslab[i]
        src, Rsrc = self.slab_src(wname, g)
        S.aop('sp', lambda e: e.dma_start(out=buf[:], in_=src), reads=[Rsrc], writes=[R])
        return buf, R

    def next_mm(self):
        i = self.mm_i
        self.mm_i = (i + 1) % len(self.mmb)
        return self.mmb[i], self.Rmm[i]

    def norm_adaln(self, xt, Rxt, Tn, A, B, h, Rh):
        S = self.S
        sq, Rsq = self.sq, self.Ru
        pst, Rpst = self.pst, self.Rpst
        ones = self.ones_bf
        S.op('act', lambda e: e.activation(out=sq[:, :, 0:Tn], in_=xt, func=AF.Square), reads=[Rxt], writes=[Rsq])

        def g(e):
            for c in range(DC):
                ins = e.matmul(pst[:, 0:Tn], lhsT=ones[:], rhs=sq[:, c, 0:Tn], start=(c == 0), stop=(c == DC - 1))
            return ins
        S.op('pe', g, reads=[Rsq, self.Rcst], writes=[Rpst])
        rs, Rrs = self.tmp('rs')
        S.op('act', lambda e: e.activation(out=rs[:, 0:Tn], in_=pst[:, 0:Tn], func=AF.Ln, scale=1.0 / D, bias=self.eps_ap),
             reads=[Rpst, self.Rcst], writes=[Rrs])
        rstd, Rrstd = self.tmp('rstd')
        S.op('act', lambda e: e.activation(out=rstd[:, 0:Tn], in_=rs[:, 0:Tn], func=AF.Exp, scale=-0.5), reads=[Rrs], writes=[Rrstd])
        for c in range(DC):
            tm, Rtm = self.tmp('nt%d' % (c % 2))
            S.op('dve', lambda e, c=c, tm=tm: e.tensor_tensor(out=tm[:, 0:Tn], in0=xt[:, c, :], in1=rstd[:, 0:Tn], op=ALU.mult),
                 reads=[Rxt, Rrstd], writes=[Rtm])
            S.op('act', lambda e, c=c, tm=tm: e.activation(out=h[:, c, 0:Tn], in_=tm[:, 0:Tn], func=AF.Identity,
                                                           scale=A[:, c:c + 1], bias=B[:, c:c + 1]),
                 reads=[Rtm, self.Rcoef], joins=[Rh])

    def tmp(self, name):
        return self.tmps[name], self.Rtmps[name]

    def mm_group(self, ps, Rps, slab, Rslab, kchunks, lhs_fn, rhs_fn, reads):
        def g(e):
            n = len(kchunks)
            for i, kc in enumerate(kchunks):
                ins = e.matmul(ps, lhsT=lhs_fn(kc), rhs=rhs_fn(kc), start=(i == 0), stop=(i == n - 1))
            return ins
        self.S.op('pe', g, reads=[Rslab] + list(reads), writes=[Rps])

    def mlp(self, l, xt, Rxt, store_fn=None):
        S = self.S
        h, Rh, u, Ru = self.h, self.Rh, self.u, self.Ru
        cf = self.coef
        self.norm_adaln(xt[:, :, :], Rxt, T, cf[:, l, 4, :], cf[:, l, 3, :], h, Rh)
        for g in range(16):
            slab, Rs = self.load_slab('w1_%d' % l, g)
            for mm in range(4):
                m = g * 4 + mm
                ps, Rps = self.next_mm()
                self.mm_group(ps[:, :], Rps, slab, Rs, range(DC),
                              lambda kc, mm=mm, slab=slab: slab[:, kc // 8, (kc % 8) * 512 + mm * P:(kc % 8) * 512 + (mm + 1) * P],
                              lambda kc: h[:, kc, :], [Rh])
                sqt, Rsqt = self.tmp('sq%d' % (m % 2))
                S.op('act', lambda e, ps=ps, sqt=sqt: e.activation(out=sqt[:, :], in_=ps[:, :], func=AF.Square),
                     reads=[Rps], writes=[Rsqt])
                S.op('dve', lambda e, ps=ps, sqt=sqt, m=m: e.scalar_tensor_tensor(
                    out=u[:, m, :], in0=ps[:, :], scalar=0.0, in1=sqt[:, :], op0=ALU.is_gt, op1=ALU.mult),
                    reads=[Rps, Rsqt], joins=[Ru])
        for g in range(16):
            slab, Rs = self.load_slab('w2_%d' % l, g)
            ps, Rps = self.next_mm()
            self.mm_group(ps[:, :], Rps, slab, Rs, range(FC),
                          lambda kc, slab=slab: slab[:, kc // 32, (kc % 32) * P:(kc % 32 + 1) * P],
                          lambda kc: u[:, kc, :], [Ru])
            S.op('dve', lambda e, ps=ps, g=g: e.scalar_tensor_tensor(
                out=xt[:, g, :], in0=ps[:, :], scalar=cf[:, l, 5, g:g + 1], in1=xt[:, g, :], op0=ALU.mult, op1=ALU.add),
                reads=[Rps, self.Rcoef, Rxt], joins=[Rxt])
            if store_fn is not None:
                store_fn(g)

    def conv_glu(self, cj, Tn, dst_fn):
        S = self.S
        h, Rh = self.h, self.Rh
        vec = self.vec
        oa, og = VOFF['cbia%d' % cj], VOFF['cbig%d' % cj]
        for g in range(8):
            slab, Rs = self.load_slab('cin_%d' % cj, g)
            for cc in range(2):
                c = 2 * g + cc
                psa, Rpa = self.next_mm()
                psg, Rpg = self.next_mm()
                for (ps, Rps, col) in ((psa, Rpa, cc), (psg, Rpg, 2 + cc)):
                    self.mm_group(ps[:, 0:Tn], Rps, slab, Rs, range(DC),
                                  lambda kc, col=col, slab=slab: slab[:, kc // 8, (kc % 8) * 512 + col * P:(kc % 8) * 512 + (col + 1) * P],
                                  lambda kc: h[:, kc, 0:Tn], [Rh])
                sg, Rsg = self.tmp('sg%d' % (c % 2))
                S.op('act', lambda e, psg=psg, sg=sg, c=c: e.activation(out=sg[:, 0:Tn], in_=psg[:, 0:Tn], func=AF.Sigmoid,
                                                                       bias=vec[:, og + c:og + c + 1], scale=1.0),
                     reads=[Rpg, self.Rvec], writes=[Rsg])
                dst, Rd, kind = dst_fn(c)
                S.op('dve', lambda e, psa=psa, sg=sg, c=c, dst=dst: e.scalar_tensor_tensor(
                    out=dst, in0=psa[:, 0:Tn], scalar=vec[:, oa + c:oa + c + 1], in1=sg[:, 0:Tn], op0=ALU.add, op1=ALU.mult),
                    reads=[Rpa, Rsg, self.Rvec], **{kind: [Rd]})
                yield c

    def conv_halo(self, l, cj, halo_src, Rsrc):
        S = self.S
        xh, Rxh = self.tmps['xh'], self.Rtmps['xh']
        S.aop('sp', lambda e: e.dma_start(out=xh[:, :, :], in_=halo_src), reads=[Rsrc], writes=[Rxh])
        cf = self.coef
        self.norm_adaln(xh[:, :, :], Rxh, HALO, cf[:, l, 1, :], cf[:, l, 0, :], self.h, self.Rh)
        carry, Rc = self.carry, self.Rcarry
        fo = VOFF['flag']
        for c in self.conv_glu(cj, HALO, lambda c: (self.carry[:, c, :], self.Rcarry, 'joins')):
            pass
        S.op('dve', lambda e: e.tensor_scalar(out=carry[:, :, :], in0=carry[:, :, :], scalar1=self.vec[:, fo:fo + 1], scalar2=None,
                                             op0=ALU.mult), reads=[self.Rvec], writes=[Rc])

    def conv_mixer(self, l, cj, xt, Rxt):
        S = self.S
        vec, cf = self.vec, self.coef
        h, Rh, u, Ru = self.h, self.Rh, self.u, self.Ru
        carry, Rc = self.carry, self.Rcarry
        v = self.v
        vb, v2b = self.vb, self.v2b
        self.norm_adaln(xt[:, :, :], Rxt, T, cf[:, l, 1, :], cf[:, l, 0, :], h, Rh)
        cbo = self.cb
        for c in range(DC):
            S.op('act', lambda e, c=c: e.activation(out=xt[:, c, :], in_=xt[:, c, :], func=AF.Identity,
                                                    bias=cbo[:, cj, c:c + 1], scale=1.0),
                 reads=[self.Rcoef, Rxt], joins=[Rxt])
        ow, ob = VOFF['cdww%d' % cj], VOFF['cdwb%d' % cj]
        ucs = {}

        def dst_fn(c):
            uc, Ruc = self.tmp('uc%d' % (c % 2))
            ucs[c] = (uc, Ruc)
            return uc[:, HALO:HALO + T], Ruc, 'writes'
        def emit_fir(c):
            uc, Ruc = ucs[c]
            S.op('act', lambda e, uc=uc, c=c: e.copy(out=uc[:, 0:HALO], in_=carry[:, c, :]), reads=[Rc], joins=[Ruc])
            fir, Rfir = self.firb[c % 2], self.Rfir[c % 2]
            dg, Rdg = self.dgb[c % 2], self.Rdg[c % 2]
            wsl = vec[:, ow + c * CW:ow + (c + 1) * CW]
            S.op('dve', lambda e, dg=dg, wsl=wsl: e.tensor_tensor(
                out=dg[:, :, :], in0=self.ident_bf[:, :].unsqueeze(1).broadcast_to([P, CW, P]),
                in1=wsl.unsqueeze(2).broadcast_to([P, CW, P]), op=ALU.mult),
                reads=[self.Rvec, self.Rcst], writes=[Rdg])

            def gfir(e, uc=uc, fir=fir, dg=dg):
                for j in range(CW):
                    ins = e.matmul(fir[:, :], lhsT=dg[:, j, :], rhs=uc[:, 2 + j:2 + j + T], start=(j == 0), stop=(j == CW - 1))
                return ins
            S.op('pe', gfir, reads=[Ruc, Rdg], writes=[Rfir])
            S.op('act', lambda e, fir=fir, c=c: e.activation(out=v[:, c, :], in_=fir[:, :], func=AF.Identity,
                                                            bias=vec[:, ob + c:ob + c + 1], scale=1.0),
                 reads=[Rfir, self.Rvec], joins=[Ru])
            S.op('act', lambda e, uc=uc, c=c: e.copy(out=carry[:, c, :], in_=uc[:, T:T + HALO]), reads=[Ruc], joins=[Rc])
        prev = None
        for c in self.conv_glu(cj, T, dst_fn):
            if prev is not None:
                emit_fir(prev)
            prev = c
        emit_fir(prev)
        Rvb = self.Rvb
        S.op('act', lambda e: e.copy(out=vb[:, :, :], in_=v[:, :, :]), reads=[Ru], joins=[Rvb])
        S.op('act', lambda e: e.activation(out=v2b[:, :, :], in_=v[:, :, :], func=AF.Square), reads=[Ru], joins=[Rvb])
        ones = self.ones_bf
        ps1, Rp1 = self.pst, self.Rpst
        ps2, Rp2 = self.pst2, self.Rpst2
        for (ps, Rps, src) in ((ps1, Rp1, vb), (ps2, Rp2, v2b)):
            def g(e, ps=ps, src=src):
                for c in range(DC):
                    ins = e.matmul(ps[:, :], lhsT=ones[:], rhs=src[:, c, :], start=(c == 0), stop=(c == DC - 1))
                return ins
            S.op('pe', g, reads=[Rvb, self.Rcst], writes=[Rps])
        mean, Rmean = self.tmp('rs')
        S.op('act', lambda e: e.activation(out=mean[:, :], in_=ps1[:, :], func=AF.Identity, scale=1.0 / D), reads=[Rp1], writes=[Rmean])
        m2, Rm2 = self.tmp('nt0')
        S.op('dve', lambda e: e.tensor_tensor(out=m2[:, :], in0=mean[:, :], in1=mean[:, :], op=ALU.mult), reads=[Rmean], writes=[Rm2])
        var, Rvar = self.tmp('nt1')
        S.op('dve', lambda e: e.scalar_tensor_tensor(out=var[:, :], in0=ps2[:, :], scalar=1.0 / D, in1=m2[:, :],
                                                     op0=ALU.mult, op1=ALU.subtract), reads=[Rp2, Rm2], writes=[Rvar])
        sd, Rsd = self.tmp('sg0')
        S.op('act', lambda e: e.activation(out=sd[:, :], in_=var[:, :], func=AF.Ln, bias=self.eps_ap, scale=1.0),
             reads=[Rvar, self.Rcst], writes=[Rsd])
        rstd, Rrstd = self.tmp('rstd')
        S.op('act', lambda e: e.activation(out=rstd[:, :], in_=sd[:, :], func=AF.Exp, scale=-0.5), reads=[Rsd], writes=[Rrstd])
        nmr, Rnmr = self.tmp('sg1')
        S.op('dve', lambda e: e.scalar_tensor_tensor(out=nmr[:, :], in0=mean[:, :], scalar=-1.0, in1=rstd[:, :],
                                                     op0=ALU.mult, op1=ALU.mult), reads=[Rmean, Rrstd], writes=[Rnmr])
        og, obb = VOFF['clng%d' % cj], VOFF['clnb%d' % cj]
        for c in range(DC):
            t1, Rt1 = self.tmp('sq%d' % (c % 2))
            S.op('dve', lambda e, c=c, t1=t1: e.tensor_tensor(out=t1[:, :], in0=v[:, c, :], in1=rstd[:, :], op=ALU.mult),
                 reads=[Ru, Rrstd], writes=[Rt1])
            S.op('dve', lambda e, t1=t1: e.tensor_tensor(out=t1[:, :], in0=t1[:, :], in1=nmr[:, :], op=ALU.add),
                 reads=[Rnmr], writes=[Rt1])
            S.op('act', lambda e, c=c, t1=t1: e.activation(out=h[:, c, :], in_=t1[:, :], func=AF.Silu,
                                                           scale=vec[:, og + c:og + c + 1], bias=vec[:, obb + c:obb + c + 1]),
                 reads=[Rt1, self.Rvec], writes=[Rh] if c == 0 else (), joins=[Rh] if c else ())
        for g in range(4):
            slab, Rs = self.load_slab('cout_%d' % cj, g)
            for mm in range(4):
                m = g * 4 + mm
                ps, Rps = self.next_mm()
                self.mm_group(ps[:, :], Rps, slab, Rs, range(DC),
                              lambda kc, mm=mm, slab=slab: slab[:, kc // 8, (kc % 8) * 512 + mm * P:(kc % 8) * 512 + (mm + 1) * P],
                              lambda kc: h[:, kc, :], [Rh])
                S.op('dve', lambda e, ps=ps, m=m: e.scalar_tensor_tensor(
                    out=xt[:, m, :], in0=ps[:, :], scalar=cf[:, l, 2, m:m + 1], in1=xt[:, m, :], op0=ALU.mult, op1=ALU.add),
                    reads=[Rps, self.Rcoef, Rxt], joins=[Rxt])

    def phase_prep(self):
        nc, S = self.nc, self.S
        with ExitStack() as ph:
            self.prep_bundle(0)
            self.prep_bundle(1)
            self.prep_bundle(2)
            vec, Rvec = self.vec, self.Rvec
            vin = self.inp('vecs', [P, NV])
            S.aop('sp', lambda e: e.dma_start(out=vec[:, :], in_=vin.ap()), writes=[Rvec])
            cin = self.inp('cst', [P, NCST])
            ctmp = self.sb(ph, 'ctmp', [P, 384], F32)
            Rct = Res('ctmp')
            S.aop('sp', lambda e: e.dma_start(out=ctmp[:, :], in_=cin.ap()[:, 0:384]), writes=[Rct])
            S.op('dve', lambda e: e.tensor_copy(out=self.ident[:, :], in_=ctmp[:, 0:128]), reads=[Rct], joins=[self.Rcst])
            S.op('dve', lambda e: e.tensor_copy(out=self.tri_bf[:, :], in_=ctmp[:, 128:256]), reads=[Rct], joins=[self.Rcst])
            S.op('dve', lambda e: e.tensor_copy(out=self.ones_bf[:, :], in_=ctmp[:, 256:384]), reads=[Rct], joins=[self.Rcst])
            S.op('dve', lambda e: e.memset(self.eps_ap, EPS), joins=[self.Rcst])
            S.op('dve', lambda e: e.tensor_copy(out=self.ident_bf[:, :], in_=ctmp[:, 0:128]), reads=[Rct], joins=[self.Rcst])
            S.op('dve', lambda e: e.tensor_scalar(out=self.trineg[:, :], in0=ctmp[:, 128:256], scalar1=-1.0, scalar2=-NEGM, op0=ALU.add, op1=ALU.mult),
                 reads=[Rct], joins=[self.Rcst])
            S.op('dve', lambda e: e.memset(self.one_ap, 1.0), joins=[self.Rcst])
            fo_ = VOFF['flag']
            S.op('dve', lambda e: e.tensor_scalar(out=self.nflag[:, :], in0=vec[:, fo_:fo_ + 1], scalar1=-1.0, scalar2=1.0, op0=ALU.mult, op1=ALU.add),
                 reads=[Rvec], joins=[self.Rcoef])
            ob_ = VOFF['modb']
            cond = self.sb(ph, 'cond', [P, 16], F32)
            Rcond = Res('cond')
            oc = VOFF['c']
            S.op('act', lambda e: e.activation(out=cond[:, :], in_=vec[:, oc:oc + 16], func=AF.Silu), reads=[Rvec], writes=[Rcond])
            nomod = 'nomod' in self.opts
            mw = None if nomod else self.inp('mod_w_h', [4, D, 6144])
            wb = [self.sb(ph, 'mwb%d' % i, [P, 6144], F32) for i in range(2)]
            Rwb = [Res('mwb%d' % i) for i in range(2)]
            psm = self.ps(ph, 'psm')
            Rpsm = PRes('psm')
            it = 0
            if nomod:
                S.op('pe', lambda e: e.matmul(psm[:, 0:192], lhsT=self.ident[:, :], rhs=vec[:, ob_:ob_ + 192], start=True, stop=True),
                     reads=[Rvec, self.Rcst], writes=[Rpsm])
            for l in range(0 if nomod else 4):
                for k in range(DC):
                    b, Rb = wb[it % 2], Rwb[it % 2]
                    S.aop('sp', lambda e, b=b, l=l, k=k: e.dma_start(out=b[:, :], in_=mw.ap()[l, k * P:(k + 1) * P, :]), writes=[Rb])

                    def g(e, b=b, l=l, k=k, it=it):
                        for m in range(48):
                            ins = e.matmul(psm[:, l * 48 + m:l * 48 + m + 1], lhsT=b[:, m * P:(m + 1) * P], rhs=cond[:, k:k + 1],
                                           start=(it == 0 and m == 0), stop=(it == 63 and m == 47), skip_group_check=True)
                        return ins
                    S.op('pe', g, reads=[Rb, Rcond], joins=[Rpsm])
                    it += 1
            modh = self.sb(ph, 'modh', [P, 192], F32)
            Rmodh = Res('modh')
            ob = VOFF['modb']
            S.op('dve', lambda e: e.tensor_tensor(out=modh[:, :], in0=psm[:, 0:192], in1=vec[:, ob:ob + 192], op=ALU.add),
                 reads=[Rpsm, Rvec], writes=[Rmodh])
            msend = self.dram('modsend', [P, 192], F32)
            mfull = self.dram('modfull', [2 * P, 192], F32)
            Rms, Rmf = Res('ms'), Res('mf')
            S.aop('sp', lambda e: e.dma_start(out=msend.ap(), in_=modh[:, :]), reads=[Rmodh], writes=[Rms])
            self.allgather(msend, mfull, Rms, Rmf)
            modt = self.sb(ph, 'modt', [P, 2, 192], F32)
            Rmodt = Res('modt')
            S.aop('sp', lambda e: e.dma_start(out=modt[:, :, :], in_=mfull.ap().rearrange("(k p) c -> p k c", k=2)),
                  reads=[Rmf], writes=[Rmodt])
            cf, Rcf = self.coef, self.Rcoef
            for l in range(4):
                for hf_, (jb, ja, jg, gname) in enumerate(((0, 1, 2, 'n1g%d'), (3, 4, 5, 'n2g%d'))):
                    base = l * 48
                    go = VOFF[gname % l]
                    S.op('dve', lambda e, l=l, hf_=hf_, jb=jb, base=base: e.tensor_copy(out=cf[:, l, jb, :], in_=modt[:, hf_, base:base + 16]),
                         reads=[Rmodt], joins=[Rcf])
                    S.op('dve', lambda e, l=l, hf_=hf_, ja=ja, base=base, go=go: e.scalar_tensor_tensor(
                        out=cf[:, l, ja, :], in0=modt[:, hf_, base + 16:base + 32], scalar=1.0, in1=vec[:, go:go + 16],
                        op0=ALU.add, op1=ALU.mult), reads=[Rmodt, Rvec], joins=[Rcf])
                    S.op('dve', lambda e, l=l, hf_=hf_, jg=jg, base=base: e.tensor_scalar(
                        out=cf[:, l, jg, :], in0=modt[:, hf_, base + 32:base + 48], scalar1=1.0, scalar2=None, op0=ALU.add),
                        reads=[Rmodt], joins=[Rcf])
            for cj, l in ((0, 0), (1, 3)):
                bo = VOFF['cbo%d' % cj]
                S.op('dve', lambda e, cj=cj, l=l, bo=bo: e.tensor_tensor(out=self.cb[:, cj, :], in0=cf[:, l, 2, :], in1=vec[:, bo:bo + 16], op=ALU.mult),
                     reads=[Rcf, Rvec], joins=[Rcf])
            S.flush('prep')

    def alloc_tok(self, ph, nslab=3):
        self.xt = self.sb(ph, 'xt', [P, DC, T], F32)
        self.Rxt = Res('xt')
        self.h = self.sb(ph, 'h', [P, DC, T], BF16)
        self.Rh = Res('h')
        self.u = self.sb(ph, 'u', [P, FC, T], BF16)
        self.Ru = Res('u')
        self.sq = self.u[:, 0:DC, :]
        self.slabs = [self.sb(ph, 'slab%d' % i, [P, 2, 4096], BF16) for i in range(nslab)]
        self.Rslab = [Res('slab%d' % i) for i in range(nslab)]
        self.slab_i = 0
        self.tmps, self.Rtmps = {}, {}
        for nm in ('rs', 'rstd', 'nt0', 'nt1', 'sq0', 'sq1', 'sg0', 'sg1'):
            self.tmps[nm] = self.sb(ph, 't_' + nm, [P, T], F32)
            self.Rtmps[nm] = Res(nm)
        self.mmb = [self.ps(ph, 'mm%d' % i) for i in range(4)]
        self.Rmm = [PRes('mm%d' % i) for i in range(4)]
        self.mm_i = 0
        self.pst = self.ps(ph, 'pst')
        self.Rpst = PRes('pst')

    def alloc_conv(self, ph):
        nc = self.nc
        for nm in ('uc0', 'uc1'):
            self.tmps[nm] = self.sb(ph, 't_' + nm, [P, HALO + T], BF16)
            self.Rtmps[nm] = Res(nm)
        self.dgb = [self.sb(ph, 'dg%d' % i, [P, CW, P], BF16) for i in range(2)]
        self.Rdg = [Res() for i in range(2)]
        self.tmps['xh'] = self.sb(ph, 't_xh', [P, DC, HALO], F32)
        self.Rtmps['xh'] = Res('xh')
        self.carry = self.sb(ph, 'carry', [P, DC, HALO], BF16)
        self.Rcarry = Res('carry')
        self.firb = [self.ps(ph, 'fir%d' % i) for i in range(2)]
        self.Rfir = [PRes('fir%d' % i) for i in range(2)]
        self.pst2 = self.ps(ph, 'pst2')
        self.Rpst2 = PRes('pst2')
        uu = self.u
        flat = uu[:, :, :].rearrange("p a b -> p (a b)")
        self.v = flat[:, 0:2 * DC * T].bitcast(F32).rearrange("p (a b) -> p a b", b=T)
        self.vb = uu[:, 32:48, :]
        self.v2b = uu[:, 48:64, :]
        self.Rvb = self.Ru

    def tile_io(self, src2d, i, n=T, c0=None):
        c0 = i * T if c0 is None else c0
        return src2d.rearrange("(c p) t -> p c t", p=P)[:, :, c0:c0 + n]

    def phase_conv_layer(self, l, cj, x_src, Rsrc, halo_src, Rhalo, x_dst, Rdst, emit_next):
        S = self.S
        with ExitStack() as ph:
            self.alloc_tok(ph)
            self.alloc_conv(ph)
            xt, Rxt = self.xt, self.Rxt
            self.conv_halo(l, cj, halo_src, Rhalo)
            for i in range(NT):
                S.aop('sp', lambda e, i=i: e.dma_start(out=xt[:, :, :], in_=self.tile_io(x_src, i)), reads=[Rsrc], writes=[Rxt])
                self.conv_mixer(l, cj, xt, Rxt)
                def store_fn(g, i=i):
                    S.aop('sp', lambda e: e.dma_start(out=self.tile_io(x_dst, i)[:, g, :], in_=xt[:, g, :]), reads=[Rxt], joins=[Rdst])
                self.mlp(l, xt, Rxt, store_fn)
                if emit_next is not None:
                    emit_next(i, xt, Rxt)
            S.flush('conv%d' % l)

    def emit_hsend(self, lnext):
        cf = self.coef

        def f(i, xt, Rxt):
            S = self.S
            self.norm_adaln(xt[:, :, :], Rxt, T, cf[:, lnext, 1, :], cf[:, lnext, 0, :], self.h, self.Rh)
            hs = self.hsend[i]
            S.aop('sp', lambda e: e.dma_start(out=hs.ap().rearrange("(c p) t -> p c t", p=P), in_=self.h[:, :, :]), reads=[self.Rh], writes=[self.Rhsend[i]])
            self.allgather(hs, self.hfull[i], self.Rhsend[i], self.Rhfull[i])
        return f

    def phase_proj(self, kind):
        nc, S = self.nc, self.S
        moba = (kind == 'moba')
        wname = 'mqkv' if moba else 'fw4'
        nsl = 6 if moba else 8
        vec = self.vec
        qg_o, kg_o = (VOFF['mqg'], VOFF['mkg']) if moba else (VOFF['fqg'], VOFF['fkg'])
        SCALE = float(P) ** -0.5
        with ExitStack() as ph:
            wres = [self.sb(ph, 'wres%d' % i, [P, 2, 4096], BF16) for i in range(nsl)]
            Rw = Res('wres')
            for i in range(nsl):
                src, Rsrc = self.slab_src(wname, i)
                S.aop('sp', lambda e, i=i, src=src: e.dma_start(out=wres[i][:], in_=src), reads=[Rsrc], joins=[Rw])
            nhb = 2 if moba else 1
            hb = [self.sb(ph, 'hb%d' % i, [P, DC, T], BF16) for i in range(nhb)]
            Rhb = [Res() for i in range(nhb)]
            tm = {}
            Rt = {}
            for nm, dt in (('raw0', F32), ('raw1', F32), ('rs0', F32), ('rs1', F32), ('rstd0', F32), ('rstd1', F32),
                           ('kn0', F32), ('kn1', F32), ('sqb0', BF16), ('sqb1', BF16), ('ob0', BF16), ('ob1', BF16),
                           ('vb0', BF16), ('vb1', BF16)):
                tm[nm] = self.sb(ph, 'p_' + nm, [P, T], dt)
                Rt[nm] = Res(nm)
            mmb = [self.ps(ph, 'mm%d' % i) for i in range(4)]
            Rmm = [PRes() for i in range(4)]
            stb = [self.ps(ph, 'st%d' % i) for i in range(2)]
            Rst = [PRes() for i in range(2)]
            psx = self.ps(ph, 'psx')
            Rpsx = PRes('psx')
            pstr = self.ps(ph, 'pstr')
            Rpstr = PRes('pstr')
            mmi = [0]

            def next_mm():
                i = mmi[0]
                mmi[0] = (i + 1) % 4
                return mmb[i], Rmm[i]
            ones = self.ones_bf
            if moba:
                kmean = self.sb(ph, 'kmean', [P, NH, 32], F32)
                Rkm = Res('kmean')
                S.op('dve', lambda e: e.memset(kmean[:, :, :], 0.0), writes=[Rkm])
                gsb4 = self.sb(ph, 'gsb4', [P, 4, 32], F32)
                Rgsb = Res('gsb')
                mx84 = self.sb(ph, 'mx84', [P, 4, 8], F32)
                Rmx = Res('mx8')
                negm4 = self.sb(ph, 'negm4', [P, 12, P], F32)
                Rnegm4 = [Res() for i in range(3)]
                S.op('dve', lambda e: e.memset(negm4[:, :, :], 0.0), writes=Rnegm4)
                negb = [self.sb(ph, 'negb%d' % i, [P, T], BF16) for i in range(2)]
                Rnegb = [Res() for i in range(2)]
            else:
                wf = self.sb(ph, 'wf', [P, DC, P], BF16)
                Rwf = Res('wf')
                S.op('dve', lambda e: e.memset(wf[:, :, :], 0.0), writes=[Rwf])
                wfi = self.inp('fox_wf', [D, 8])
                S.aop('pool', lambda e: e.dma_start(out=wf[:, :, 0:8], in_=wfi.ap().rearrange("(c p) n -> p c n", p=P)),
                      reads=[], joins=[Rwf])
                nbf = self.sb(ph, 'nbf', [P, 1], F32)
                Rnbf = Res('nbf')
                fo = VOFF['fbf']
                S.op('dve', lambda e: e.tensor_scalar(out=nbf[:, :], in0=vec[:, fo:fo + 1], scalar1=-1.0, scalar2=None, op0=ALU.mult),
                     reads=[self.Rvec], writes=[Rnbf])
                onesf = self.sb(ph, 'onesf', [P, T], F32)
                Ronesf = Res('onesf')
                S.op('dve', lambda e: e.memset(onesf[:, :], 1.0), writes=[Ronesf])
                cumt = [self.sb(ph, 'cumt%d' % i, [P, T], F32) for i in range(2)]
                Rcum = [Res() for i in range(2)]
                c3b = [self.sb(ph, 'c3b%d' % i, [P, T], BF16) for i in range(3)]
                Rc3b = [Res() for i in range(3)]
                r1 = self.sb(ph, 'r1', [P, T], F32)
                Rr1 = Res('r1')

            def head_A(hbuf, Rh_, slab_i, colblk):
                it = head_A.it
                head_A.it += 1
                ps, Rps = next_mm()
                w = wres[slab_i]

                def gmm(e):
                    for kc in range(DC):
                        ins = e.matmul(ps[:, :], lhsT=w[:, kc // 8, (kc % 8) * 512 + colblk * P:(kc % 8) * 512 + (colblk + 1) * P],
                                       rhs=hbuf[:, kc, :], start=(kc == 0), stop=(kc == DC - 1))
                    return ins
                S.op('pe', gmm, reads=[Rw, Rh_], writes=[Rps])
                sqb, Rsqb = tm['sqb%d' % (it % 2)], Rt['sqb%d' % (it % 2)]
                raw, Rraw = tm['raw%d' % (it % 2)], Rt['raw%d' % (it % 2)]
                S.op('act', lambda e: e.activation(out=sqb[:, :], in_=ps[:, :], func=AF.Square), reads=[Rps], writes=[Rsqb])
                S.op('dve', lambda e: e.tensor_copy(out=raw[:, :], in_=ps[:, :]), reads=[Rps], writes=[Rraw])
                return it

            def head_B(it, g_off, scale, dst_dram_rows, g):
                sqb, Rsqb = tm['sqb%d' % (it % 2)], Rt['sqb%d' % (it % 2)]
                raw, Rraw = tm['raw%d' % (it % 2)], Rt['raw%d' % (it % 2)]
                st, Rs_ = stb[it % 2], Rst[it % 2]
                S.op('pe', lambda e: e.matmul(st[:, :], lhsT=ones[:], rhs=sqb[:, :], start=True, stop=True),
                     reads=[Rsqb, self.Rcst], writes=[Rs_])
                rs, Rrs = tm['rs%d' % (it % 2)], Rt['rs%d' % (it % 2)]
                S.op('act', lambda e: e.activation(out=rs[:, :], in_=st[:, :], func=AF.Ln, scale=1.0 / P, bias=self.eps_ap),
                     reads=[Rs_, self.Rcst], writes=[Rrs])
                rstd, Rrstd = tm['rstd%d' % (it % 2)], Rt['rstd%d' % (it % 2)]
                S.op('act', lambda e: e.activation(out=rstd[:, :], in_=rs[:, :], func=AF.Exp, scale=-0.5), reads=[Rrs], writes=[Rrstd])
                kn, Rkn = tm['kn%d' % (it % 2)], Rt['kn%d' % (it % 2)]
                S.op('dve', lambda e: e.scalar_tensor_tensor(out=kn[:, :], in0=raw[:, :], scalar=vec[:, g_off:g_off + 1], in1=rstd[:, :],
                                                             op0=ALU.mult, op1=ALU.mult), reads=[Rraw, Rrstd, self.Rvec], writes=[Rkn])
                ob, Rob = tm['ob%d' % (it % 2)], Rt['ob%d' % (it % 2)]
                S.op('act', lambda e: e.activation(out=ob[:, :], in_=kn[:, :], func=AF.Identity, scale=scale), reads=[Rkn], writes=[Rob])
                S.aop('sp', lambda e: e.dma_start(out=dst_dram_rows[:, g * T:(g + 1) * T], in_=ob[:, :]), reads=[Rob])
                return kn, Rkn
            head_A.it = 0
            deferred = []

            def load_h(g):
                hbuf, Rh_ = hb[g % nhb], Rhb[g % nhb]
                rank, i = g // NT, g % NT
                S.aop('sp', lambda e: e.dma_start(
                    out=hbuf[:, :, :], in_=self.hfull[i].ap()[rank * D:(rank + 1) * D, :].rearrange("(c p) t -> p c t", p=P)),
                    reads=[self.Rhfull[i]], writes=[Rh_])
            load_h(0)
            for g in range(2 * NT):
                hbuf, Rh_ = hb[g % nhb], Rhb[g % nhb]
                if nhb == 2 and g + 1 < 2 * NT:
                    load_h(g + 1)
                elif nhb == 1 and g > 0:
                    load_h(g)
                jobs = [('k', hd) for hd in range(NH)] + [('q', hd) for hd in range(NH)]

                def job_A(job):
                    kq, hd = job
                    return head_A(hbuf, Rh_, (2 if kq == 'k' else 0) + hd // 4, hd % 4)
                it_cur = job_A(jobs[0])
                for ji, (kq, hd) in enumerate(jobs):
                    it_next = job_A(jobs[ji + 1]) if ji + 1 < len(jobs) else None
                    if kq == 'k':
                        kn, Rkn = head_B(it_cur, kg_o, 1.0, self.KT.ap()[hd * P:(hd + 1) * P, :], g)
                        it_cur = it_next
                        if moba and 'pj_nored' not in self.opts:
                            S.op('dve', lambda e, kn=kn, hd=hd, g=g: e.tensor_reduce(
                                out=kmean[:, hd, 2 * g:2 * g + 2], in_=kn[:, :].rearrange("p (a b) -> p a b", b=256), axis=AX.X, op=ALU.add),
                                reads=[Rkn], joins=[Rkm])
                        continue
                    qn, Rqn = head_B(it_cur, qg_o, SCALE, self.QT.ap()[hd * P:(hd + 1) * P, :], g)
                    it_cur = it_next
                    if not moba or 'pj_nogate' in self.opts:
                        continue
                    nb_i = (g * NH + hd) % 2
                    hp = hd % 2

                    hp = hd % 3
                    curs = [2 * g + sub // 2 for sub in range(4)]
                    nm4 = negm4[:, hp * 4:(hp + 1) * 4, :]
                    Rnm = Rnegm4[hp]

                    def stage_a(hd=hd, g=g, qn=qn, Rqn=Rqn, curs=curs, nm4=nm4, Rnm=Rnm):
                        first_mm = True
                        for sub in range(4):
                            if curs[sub] > 3:
                                S.op('pe', lambda e, sub=sub: e.matmul(
                                    psx[:, sub * 32:(sub + 1) * 32], lhsT=qn[:, sub * P:(sub + 1) * P], rhs=kmean[:, hd, :], start=True, stop=True,
                                    skip_group_check=True),
                                    reads=[Rqn, Rkm], **({'writes': [Rpsx]} if first_mm else {'joins': [Rpsx]}))
                                first_mm = False
                        S.op('dve', lambda e: e.memset(nm4[:, :, 0:32], NEGM), writes=[Rnm])
                        if curs[3] > 3:
                            S.op('dve', lambda e: e.memset(gsb4[:, :, :], -1e30), writes=[Rgsb])
                        for sub in range(4):
                            cur = curs[sub]
                            if cur > 3:
                                S.op('dve', lambda e, cur=cur, sub=sub: e.tensor_copy(out=gsb4[:, sub, 0:cur], in_=psx[:, sub * 32:sub * 32 + cur]),
                                     reads=[Rpsx], joins=[Rgsb])
                                S.op('dve', lambda e, sub=sub: e.max(out=mx84[:, sub, :], in_=gsb4[:, sub, :]), reads=[Rgsb], joins=[Rmx])
                                S.op('dve', lambda e, cur=cur, sub=sub: e.tensor_scalar(
                                    out=nm4[:, sub, 0:cur], in0=gsb4[:, sub, 0:cur], scalar1=mx84[:, sub, 2:3], scalar2=NEGM, op0=ALU.is_lt, op1=ALU.mult),
                                    reads=[Rgsb, Rmx], joins=[Rnm])
                                S.op('dve', lambda e, cur=cur, sub=sub: e.memset(nm4[:, sub, cur:cur + 1], 0.0), joins=[Rnm])
                            else:
                                S.op('dve', lambda e, cur=cur, sub=sub: e.memset(nm4[:, sub, 0:cur + 1], 0.0), joins=[Rnm])

                    def stage_b(hd=hd, g=g, nb_i=nb_i, nm4=nm4, Rnm=Rnm):
                        for sub in range(4):
                            S.op('pe', lambda e, sub=sub: e.transpose(out=pstr[:, sub * P:(sub + 1) * P], in_=nm4[:, sub, :], identity=self.ident[:, :]),
                                 reads=[Rnm, self.Rcst], **({'writes': [Rpstr]} if sub == 0 else {'joins': [Rpstr]}))
                        nbuf, Rnb = negb[nb_i], Rnegb[nb_i]
                        S.op('act', lambda e: e.copy(out=nbuf[:, :], in_=pstr[:, :]), reads=[Rpstr], writes=[Rnb])
                        S.aop('sp', lambda e: e.dma_start(out=self.negT.ap()[hd * 32:(hd + 1) * 32, g * T:(g + 1) * T], in_=nbuf[0:32, :]),
                              reads=[Rnb])
                    if len(deferred) >= 2:
                        deferred.pop(0)()
                    stage_a()
                    deferred.append(stage_b)
                for fn_ in deferred:
                    fn_()
                deferred[:] = []
                vs0 = 4 if moba else 4
                for tt in range(0 if 'pj_nov' in self.opts else 4):
                    for vs in range(2):
                        ps, Rps = next_mm()
                        w = wres[vs0 + vs]

                        def gv(e, ps=ps, w=w, tt=tt, hbuf=hbuf):
                            for kc in range(DC):
                                ins = e.matmul(ps[:, :], lhsT=hbuf[:, kc, tt * P:(tt + 1) * P], rhs=w[:, kc // 8, (kc % 8) * 512:(kc % 8 + 1) * 512],
                                               start=(kc == 0), stop=(kc == DC - 1))
                            return ins
                        S.op('pe', gv, reads=[Rw, Rh_], writes=[Rps])
                        vi = (tt * 2 + vs) % 2
                        vb_, Rvb_ = tm['vb%d' % vi], Rt['vb%d' % vi]
                        S.op('act', lambda e, ps=ps, vb_=vb_: e.copy(out=vb_[:, :], in_=ps[:, :]), reads=[Rps], writes=[Rvb_])
                        r0 = g * T + tt * P
                        S.aop('sp', lambda e, vb_=vb_, r0=r0, vs=vs: e.dma_start(out=self.Vs.ap()[r0:r0 + P, vs * 512:(vs + 1) * 512], in_=vb_[:, :]),
                              reads=[Rvb_])
                if moba:
                    continue
                for ch in range(NH):
                    ps, Rps = next_mm()
                    w = wres[6 + ch // 4]
                    cb_ = ch % 4

                    def gg(e, ps=ps, w=w, cb_=cb_, hbuf=hbuf):
                        for kc in range(DC):
                            ins = e.matmul(ps[:, :], lhsT=w[:, kc // 8, (kc % 8) * 512 + cb_ * P:(kc % 8) * 512 + (cb_ + 1) * P],
                                           rhs=hbuf[:, kc, :], start=(kc == 0), stop=(kc == DC - 1))
                        return ins
                    S.op('pe', gg, reads=[Rw, Rh_], writes=[Rps])
                    ob, Rob = tm['ob%d' % (ch % 2)], Rt['ob%d' % (ch % 2)]
                    S.op('act', lambda e, ps=ps, ob=ob: e.activation(out=ob[:, :], in_=ps[:, :], func=AF.Sigmoid), reads=[Rps], writes=[Rob])
                    S.aop('sp', lambda e, ob=ob, ch=ch, g=g: e.dma_start(out=self.SGT.ap()[ch * P:(ch + 1) * P, g * T:(g + 1) * T], in_=ob[:, :]),
                          reads=[Rob])
                def gf(e, hbuf=hbuf):
                    for kc in range(DC):
                        ins = e.matmul(psx[:, :], lhsT=wf[:, kc, :], rhs=hbuf[:, kc, :], start=(kc == 0), stop=(kc == DC - 1))
                    return ins
                S.op('pe', gf, reads=[Rwf, Rh_], writes=[Rpsx])
                e1, Re1 = tm['raw0'], Rt['raw0']
                S.op('act', lambda e: e.activation(out=e1[:, :], in_=psx[:, :], func=AF.Exp, scale=-1.0, bias=nbf[:, 0:1]),
                     reads=[Rpsx, Rnbf], writes=[Re1])
                l1, Rl1 = tm['raw1'], Rt['raw1']
                S.op('act', lambda e: e.activation(out=l1[:, :], in_=e1[:, :], func=AF.Ln, bias=self.one_ap, scale=1.0),
                     reads=[Re1, self.Rcst], writes=[Rl1])
                cm, Rcm = cumt[g % 2], Rcum[g % 2]
                pv, Rpv = cumt[(g + 1) % 2], Rcum[(g + 1) % 2]
                init = 0.0 if g == 0 else pv[:, T - 1:T]
                S.op('dve', lambda e, cm=cm, init=init: e.tensor_tensor_scan(out=cm[:, :], data0=onesf[:, :], data1=l1[:, :], initial=init,
                                                                            op0=ALU.mult, op1=ALU.subtract),
                     reads=[Rl1, Ronesf] + ([Rpv] if g else []), writes=[Rcm])
                S.op('act', lambda e, cm=cm: e.copy(out=c3b[0][:, :], in_=cm[:, :]), reads=[Rcm], writes=[Rc3b[0]])
                S.op('dve', lambda e, cm=cm: e.tensor_tensor(out=r1[:, :], in0=cm[:, :], in1=c3b[0][:, :], op=ALU.subtract),
                     reads=[Rcm, Rc3b[0]], writes=[Rr1])
                S.op('act', lambda e: e.copy(out=c3b[1][:, :], in_=r1[:, :]), reads=[Rr1], writes=[Rc3b[1]])
                S.op('dve', lambda e: e.tensor_tensor(out=r1[:, :], in0=r1[:, :], in1=c3b[1][:, :], op=ALU.subtract),
                     reads=[Rc3b[1], Rr1], writes=[Rr1])
                S.op('act', lambda e: e.copy(out=c3b[2][:, :], in_=r1[:, :]), reads=[Rr1], writes=[Rc3b[2]])
                for k3 in range(3):
                    S.aop('sp', lambda e, k3=k3, g=g: e.dma_start(out=self.c3.ap()[k3 * 8:(k3 + 1) * 8, g * T:(g + 1) * T], in_=c3b[k3][0:8, :]),
                          reads=[Rc3b[k3]])
                for tt in range(4):
                    S.op('pe', lambda e, cm=cm, tt=tt: e.transpose(out=pstr[:, tt * P:(tt + 1) * P], in_=cm[:, tt * P:(tt + 1) * P], identity=self.ident[:, :]),
                         reads=[Rcm, self.Rcst], **({'writes': [Rpstr]} if tt == 0 else {'joins': [Rpstr]}))
                S.op('dve', lambda e, g=g: e.tensor_scalar(out=self.negcumS[:, g * 4:(g + 1) * 4, :],
                                                           in0=pstr[:, :].rearrange("p (a b) -> p a b", b=P)[:, :, 0:NH],
                                                           scalar1=-1.0, scalar2=None, op0=ALU.mult),
                     reads=[Rpstr], joins=[self.Rncs])
            S.flush('proj_' + kind)

    def phase_attn(self, kind):
        nc, S = self.nc, self.S
        moba = (kind == 'moba')
        with ExitStack() as ph:
            cin = self.inp('cst', [P, NCST])
            ne = 32 if moba else NH
            eoff = CST_E if moba else CST_EF
            etmp = self.sb(ph, 'etmp', [P, ne * P], F32)
            Retmp = Res('etmp')
            S.aop('sp', lambda e: e.dma_start(out=etmp[:, :], in_=cin.ap()[:, eoff:eoff + ne * P]), writes=[Retmp])
            Eb = self.sb(ph, 'Eb', [P, ne, P], BF16)
            REb = Res('Eb')
            S.op('dve', lambda e: e.tensor_copy(out=Eb[:, :, :].rearrange("p a b -> p (a b)"), in_=etmp[:, :]), reads=[Retmp], writes=[REb])
            qt = [self.sb(ph, 'qt%d' % i, [P, SEQ], BF16) for i in range(2)]
            kt = [self.sb(ph, 'kt%d' % i, [P, SEQ], BF16) for i in range(2)]
            vh = [self.sb(ph, 'vh%d' % i, [P, 64, P], BF16) for i in range(2)]
            Rq = [Res() for i in range(2)]
            Rk = [Res() for i in range(2)]
            Rv = [Res() for i in range(2)]
            if moba:
                bias_t = [self.sb(ph, 'ngt%d' % i, [P, SEQ], BF16) for i in range(2)]
                Rbias = [Res() for i in range(2)]
                for i in range(2):
                    S.op('pool', lambda e, i=i: e.memset(bias_t[i][:, :], 0.0), writes=[Rbias[i]])
            else:
                c3t = self.sb(ph, 'c3t', [P, SEQ], BF16)
                Rc3t = Res('c3t')
                S.op('pool', lambda e: e.memset(c3t[:, :], 0.0), writes=[Rc3t])
                S.aop('sp', lambda e: e.dma_start(out=c3t[0:24, :], in_=self.c3.ap()), reads=[], joins=[Rc3t])
                sgh = [self.sb(ph, 'sgh%d' % i, [P, SEQ], BF16) for i in range(2)]
                Rsg = [Res() for i in range(2)]
            pT = [self.sb(ph, 'pT%d' % i, [P, T], BF16) for i in range(3)]
            RpT = [Res() for i in range(3)]
            oh = [self.sb(ph, 'oh%d' % i, [P, T], BF16) for i in range(2)]
            Roh = [Res() for i in range(2)]
            rden = self.sb(ph, 'rden', [P, T], F32)
            Rrden = Res('rden')
            otmp = self.sb(ph, 'otmp', [P, T], F32)
            Rotmp = Res('otmp')
            pss = [self.ps(ph, 'pss%d' % i) for i in range(3)]
            Rpss = [PRes() for i in range(3)]
            oacc = [self.ps(ph, 'oacc%d' % i) for i in range(2)]
            Roacc = [PRes() for i in range(2)]
            dacc = [self.ps(ph, 'dacc%d' % i) for i in range(2)]
            Rdacc = [PRes() for i in range(2)]
            ones, tri = self.ones_bf, self.tri_bf
            def load_head(hd):
                b2 = hd % 2
                S.aop('sp', lambda e: e.dma_start(out=qt[b2][:, :], in_=self.QT.ap()[hd * P:(hd + 1) * P, :]), writes=[Rq[b2]])
                S.aop('sp', lambda e: e.dma_start(out=kt[b2][:, :], in_=self.KT.ap()[hd * P:(hd + 1) * P, :]), writes=[Rk[b2]])
                for part in range(4):
                    S.aop('sp', lambda e, part=part: e.dma_start(
                        out=vh[b2][:, part * 16:(part + 1) * 16, :],
                        in_=self.Vs.ap()[part * 2048:(part + 1) * 2048, hd * P:(hd + 1) * P].rearrange("(c p) d -> p c d", p=P)),
                        **({'writes': [Rv[b2]]} if part == 0 else {'joins': [Rv[b2]]}))
                if moba:
                    S.aop('sp', lambda e: e.dma_start(out=bias_t[b2][0:32, :], in_=self.negT.ap()[hd * 32:(hd + 1) * 32, :]),
                          reads=[], joins=[Rbias[b2]])
                else:
                    S.aop('sp', lambda e: e.dma_start(out=sgh[b2][:, :], in_=self.SGT.ap()[hd * P:(hd + 1) * P, :]), writes=[Rsg[b2]])

            tiles = []
            for hd in range(NH):
                for j in range(2 * NT):
                    nsc = 4 * j + 4
                    for sc in range(nsc):
                        tiles.append((hd, j, sc, nsc))

            def emit_scores(i):
                hd, j, sc, nsc = tiles[i]
                b2 = hd % 2
                q_, k_ = qt[b2], kt[b2]
                brhs = bias_t[b2] if moba else c3t
                Rb_ = Rbias[b2] if moba else Rc3t
                r = sc - 4 * j if sc >= 4 * j else 0
                c0 = r * P
                n = T - c0
                q0 = j * T + c0
                ps, Rps = pss[i % 3], Rpss[i % 3]
                pt, Rpt = pT[i % 3], RpT[i % 3]
                esel = (sc // 2) if moba else hd
                diag = sc >= 4 * j

                def gs(e):
                    e.matmul(ps[:, c0:c0 + n], lhsT=k_[:, sc * P:(sc + 1) * P], rhs=q_[:, q0:q0 + n], start=True, stop=False)
                    ins = e.matmul(ps[:, c0:c0 + n], lhsT=Eb[:, esel, :], rhs=brhs[:, q0:q0 + n], start=False, stop=not diag,
                                   skip_group_check=True)
                    if diag:
                        ins = e.matmul(ps[:, c0:c0 + P], lhsT=self.ident_bf[:, :], rhs=self.trineg[:, :], start=False, stop=True,
                                       skip_group_check=True)
                    return ins
                S.op('pe', gs, reads=[Rk[b2], Rq[b2], REb, Rb_, self.Rcst], writes=[Rps])
                if moba:
                    S.op('act', lambda e: e.activation(out=pt[:, 0:n], in_=ps[:, c0:c0 + n], func=AF.Exp), reads=[Rps], writes=[Rpt])
                else:
                    S.op('act', lambda e: e.activation(out=pt[:, 0:n], in_=ps[:, c0:c0 + n], func=AF.Exp,
                                                       bias=self.negcumS[:, sc, hd:hd + 1], scale=1.0),
                         reads=[Rps, self.Rncs], writes=[Rpt])

            def emit_pv(i):
                hd, j, sc, nsc = tiles[i]
                b2 = hd % 2
                v_ = vh[b2]
                r = sc - 4 * j if sc >= 4 * j else 0
                c0 = r * P
                n = T - c0
                pt, Rpt = pT[i % 3], RpT[i % 3]
                oa, Roa = oacc[j % 2], Roacc[j % 2]
                da, Rda = dacc[j % 2], Rdacc[j % 2]
                first, last = (sc == 0), (sc == nsc - 1)

                def gpv(e):
                    e.matmul(oa[:, c0:c0 + n], lhsT=v_[:, sc, :], rhs=pt[:, 0:n], start=first, stop=last, skip_group_check=True)
                    return e.matmul(da[:, c0:c0 + n], lhsT=ones[:, :], rhs=pt[:, 0:n], start=first, stop=last, skip_group_check=True)
                kw = {'writes': [Roa, Rda]} if first else {'joins': [Roa, Rda]}
                S.op('pe', gpv, reads=[Rv[b2], Rpt, self.Rcst], **kw)
                if not last:
                    return
                S.op('dve', lambda e: e.reciprocal(out=rden[:, :], in_=da[:, :]), reads=[Rda], writes=[Rrden])
                o_, Ro_ = oh[j % 2], Roh[j % 2]
                if moba:
                    S.op('dve', lambda e: e.tensor_tensor(out=o_[:, :], in0=oa[:, :], in1=rden[:, :], op=ALU.mult),
                         reads=[Roa, Rrden], writes=[Ro_])
                else:
                    S.op('dve', lambda e: e.tensor_tensor(out=otmp[:, :], in0=oa[:, :], in1=rden[:, :], op=ALU.mult),
                         reads=[Roa, Rrden], writes=[Rotmp])
                    S.op('dve', lambda e: e.tensor_tensor(out=o_[:, :], in0=otmp[:, :], in1=sgh[b2][:, j * T:(j + 1) * T], op=ALU.mult),
                         reads=[Rotmp, Rsg[b2]], writes=[Ro_])
                S.aop('sp', lambda e: e.dma_start(out=self.osend[hd].ap()[:, j * T:(j + 1) * T], in_=o_[:, :]),
                      reads=[Ro_], joins=[self.Rosend[hd]])
                if j == 2 * NT - 1:
                    self.allgather(self.osend[hd], self.ofull[hd], self.Rosend[hd], self.Rofull[hd])

            load_head(0)
            load_head(1)
            emit_scores(0)
            for i in range(len(tiles)):
                hd, j, sc, nsc = tiles[i]
                if j == 0 and sc == 0 and 1 <= hd < NH - 1:
                    load_head(hd + 1)
                if i + 1 < len(tiles):
                    emit_scores(i + 1)
                emit_pv(i)
            S.flush('attn_' + kind)

    def phase_tail(self, l, wo_name, x_dst, Rdst, emit_next):
        S = self.S
        cf = self.coef
        fo = VOFF['flag']
        with ExitStack() as ph:
            self.alloc_tok(ph, nslab=4)
            xt, Rxt, h, Rh, u, Ru = self.xt, self.Rxt, self.h, self.Rh, self.u, self.Ru
            cand = [u[:, 0:16, :], u[:, 16:32, :]]
            for i in range(NT):
                S.aop('sp', lambda e, i=i: e.dma_start(out=xt[:, :, :], in_=self.tile_io(self.xres.ap(), i)), reads=[self.Rxres], writes=[Rxt])
                for half in range(2):
                    for hd in range(NH):
                        c0 = half * TOK + i * T
                        S.aop('sp', lambda e, half=half, hd=hd, c0=c0: e.dma_start(
                            out=u[:, half * 16 + hd:half * 16 + hd + 9:8, :],
                            in_=self.ofull[hd].ap().rearrange("(r p) t -> p r t", r=2)[:, :, c0:c0 + T]),
                            reads=[self.Rofull[hd]], **({'writes': [Ru]} if (half == 0 and hd == 0) else {'joins': [Ru]}))
                S.op('dve', lambda e: e.tensor_scalar(out=cand[0], in0=cand[0], scalar1=self.nflag[:, 0:1], scalar2=None, op0=ALU.mult),
                     reads=[Ru, self.Rcoef], joins=[Ru])
                S.op('dve', lambda e: e.scalar_tensor_tensor(out=h[:, :, :], in0=cand[1], scalar=self.vec[:, fo:fo + 1], in1=cand[0],
                                                             op0=ALU.mult, op1=ALU.add), reads=[Ru, self.Rvec], writes=[Rh])
                for g in range(4):
                    slab, Rs = self.load_slab(wo_name, g)
                    for mm in range(4):
                        m = g * 4 + mm
                        ps, Rps = self.next_mm()
                        self.mm_group(ps[:, :], Rps, slab, Rs, range(DC),
                                      lambda kc, mm=mm, slab=slab: slab[:, kc // 8, (kc % 8) * 512 + mm * P:(kc % 8) * 512 + (mm + 1) * P],
                                      lambda kc: h[:, kc, :], [Rh])
                        S.op('dve', lambda e, ps=ps, m=m: e.scalar_tensor_tensor(
                            out=xt[:, m, :], in0=ps[:, :], scalar=cf[:, l, 2, m:m + 1], in1=xt[:, m, :], op0=ALU.mult, op1=ALU.add),
                            reads=[Rps, self.Rcoef, Rxt], joins=[Rxt])
                def store_fn(g, i=i):
                    S.aop('sp', lambda e: e.dma_start(out=self.tile_io(x_dst, i)[:, g, :], in_=xt[:, g, :]), reads=[Rxt], joins=[Rdst])
                self.mlp(l, xt, Rxt, store_fn)
                if emit_next is not None:
                    emit_next(i, xt, Rxt)
            S.flush('tail%d' % l)

    def emit_halo(self):
        def f(i, xt, Rxt):
            if i != NT - 1:
                return
            S = self.S
            S.aop('sp', lambda e: e.dma_start(out=self.halosend.ap().rearrange("(c p) t -> p c t", p=P), in_=xt[:, :, T - HALO:T]),
                  reads=[Rxt], writes=[self.Rhalosend])
            self.allgather(self.halosend, self.halofull, self.Rhalosend, self.Rhalofull)
        return f

    def build(self):
        nc = self.nc
        with ExitStack() as top:
            self.S = S = Sched(nc, top)
            self.vec = self.sb(top, 'vec', [P, NV], F32)
            self.Rvec = Res('vec')
            self.ident = self.sb(top, 'ident', [P, P], F32)
            self.tri_bf = self.sb(top, 'tri', [P, P], BF16)
            self.ones_bf = self.sb(top, 'ones', [P, P], BF16)
            self.ident_bf = self.sb(top, 'identb', [P, P], BF16)
            self.trineg = self.sb(top, 'trineg', [P, P], BF16)
            self.epst = self.sb(top, 'epst', [P, 1], F32)
            self.eps_ap = self.epst[:, 0:1]
            self.Rcst = Res('cst')
            self.coef = self.sb(top, 'coef', [P, 4, 6, 16], F32)
            self.cb = self.sb(top, 'cb', [P, 2, 16], F32)
            self.Rcoef = Res('coef')
            self.setup_weights()
            self.xres = self.dram('xres', [D, TOK], F32)
            self.Rxres = Res('xres')
            self.hsend = [self.dram('hsend%d' % i, [D, T], BF16) for i in range(NT)]
            self.hfull = [self.dram('hfull%d' % i, [2 * D, T], BF16) for i in range(NT)]
            self.Rhsend = [Res() for i in range(NT)]
            self.Rhfull = [Res() for i in range(NT)]
            xin = self.inp('xT', [D, TOK])
            halo0 = self.inp('halo0', [D, HALO])
            Rin = Res('xin')
            out = nc.dram_tensor('outT', [D, TOK], F32, kind="ExternalOutput")
            Rout = Res('out')

            self.one_t = self.sb(top, 'onet', [P, 1], F32)
            self.one_ap = self.one_t[:, 0:1]
            self.nflag = self.sb(top, 'nflag', [P, 1], F32)
            self.negcumS = self.sb(top, 'ncs', [P, 64, NH], F32)
            self.Rncs = Res('ncs')
            self.QT = self.dram('QT', [NH * P, SEQ], BF16)
            self.KT = self.dram('KT', [NH * P, SEQ], BF16)
            self.Vs = self.dram('Vs', [SEQ, NH * P], BF16)
            self.SGT = self.dram('SGT', [NH * P, SEQ], BF16)
            self.negT = self.dram('negT', [NH * 32, SEQ], BF16)
            self.c3 = self.dram('c3', [24, SEQ], BF16)
            self.osend = [self.dram('osend%d' % i, [P, SEQ], BF16) for i in range(NH)]
            self.ofull = [self.dram('ofull%d' % i, [2 * P, SEQ], BF16) for i in range(NH)]
            self.Rosend = [Res() for i in range(NH)]
            self.Rofull = [Res() for i in range(NH)]
            for nm in self.dump_names:
                self.dumps.append((nm, getattr(self, nm)))
            self.halosend = self.dram('halosend', [D, HALO], F32)
            self.halofull = self.dram('halofull', [2 * D, HALO], F32)
            self.Rhalosend, self.Rhalofull = Res(), Res()

            stop = min(self.stop_after, 7)
            final_step = {0: 0, 1: 0, 2: 0, 3: 3, 4: 3, 5: 3, 6: 6, 7: 7}[stop]

            def dst(step):
                return (out.ap(), Rout) if step == final_step else (self.xres.ap(), self.Rxres)
            self.phase_prep()
            if stop >= 1:
                self.prep_local()
            if stop >= 3:
                for bi in (3, 4, 5):
                    self.prep_bundle(bi)
            d_, R_ = dst(0)
            self.phase_conv_layer(0, 0, xin.ap(), Rin, halo0.ap().rearrange("(c p) t -> p c t", p=P), Rin,
                                  d_, R_, None if stop == 0 else self.emit_hsend(1))
            if stop >= 1:
                if stop >= 6:
                    for bi in (6, 7, 8):
                        self.prep_bundle(bi)
                if 'skip_proj' not in self.opts:
                    self.phase_proj('moba')
            if stop >= 2:
                self.phase_attn('moba')
            if stop >= 3:
                d_, R_ = dst(3)
                self.phase_tail(1, 'mwo', d_, R_, None if stop == 3 else self.emit_hsend(2))
            if stop >= 4:
                if stop >= 7:
                    for bi in (9, 10, 11):
                        self.prep_bundle(bi)
                self.phase_proj('fox')
            if stop >= 5:
                self.phase_attn('fox')
            if stop >= 6:
                d_, R_ = dst(6)
                self.phase_tail(2, 'fwo', d_, R_, None if stop == 6 else self.emit_halo())
            if stop >= 7:
                self.phase_conv_layer(3, 1, self.xres.ap(), self.Rxres,
                                      self.halofull.ap()[0:D, :].rearrange("(c p) t -> p c t", p=P), self.Rhalofull,
                                      out.ap(), Rout, None)
            for nm, t in self.dumps:
                o_ = nc.dram_tensor('dump_' + nm, list(t.shape), t.dtype, kind="ExternalOutput")
                S.aop('sp', lambda e, o_=o_, t=t: e.dma_start(out=o_.ap(), in_=t.ap()))
            S.flush('end', drain=('dma', 'bg', 'cc'))
        return nc


def _pvec(v):
    return np.ascontiguousarray(np.asarray(v, np.float32).reshape(-1, P).T)


def make_cst():
    cst = np.zeros((P, NCST), np.float32)
    cst[:, CST_IDENT:CST_IDENT + P] = np.eye(P, dtype=np.float32)
    cst[:, CST_TRI:CST_TRI + P] = np.triu(np.ones((P, P), np.float32))
    cst[:, CST_ONES:CST_ONES + P] = 1.0
    for n in range(32):
        cst[n, CST_E + n * P:CST_E + (n + 1) * P] = 1.0
    for hd in range(8):
        for r in (hd, 8 + hd, 16 + hd):
            cst[r, CST_EF + hd * P:CST_EF + (hd + 1) * P] = 1.0
    return cst


def make_vecs(inp, b, hf):
    vec = np.zeros((P, NV), np.float32)

    def put(name, arr):
        arr = np.asarray(arr, np.float32)
        if arr.ndim == 1:
            arr = arr[:, None]
        vec[:arr.shape[0], VOFF[name]:VOFF[name] + arr.shape[1]] = arr
    put('c', _pvec(inp['c'][b]))
    for l in range(4):
        put('n1g%d' % l, _pvec(inp['norm1_g'][l]))
        put('n2g%d' % l, _pvec(inp['norm2_g'][l]))
    mb = np.zeros((P, 192), np.float32)
    for l in range(4):
        for jj in range(3):
            j = hf * 3 + jj
            mb[:, l * 48 + jj * 16:l * 48 + jj * 16 + 16] = _pvec(inp['mod_b'][l, j * D:(j + 1) * D])
    put('modb', mb)
    for cj in range(2):
        put('cbia%d' % cj, _pvec(inp['conv_b_in'][cj, :D]))
        put('cbig%d' % cj, _pvec(inp['conv_b_in'][cj, D:]))
        w = np.asarray(inp['conv_dw_w'][cj], np.float32)
        put('cdww%d' % cj, np.ascontiguousarray(w.reshape(CW, 16, P).transpose(2, 1, 0)).reshape(P, 16 * CW))
        put('cdwb%d' % cj, _pvec(inp['conv_dw_b'][cj]))
        put('clng%d' % cj, _pvec(inp['conv_ln_g'][cj]))
        put('clnb%d' % cj, _pvec(inp['conv_ln_b'][cj]))
        put('cbo%d' % cj, _pvec(inp['conv_b_out'][cj]))
    put('mqg', inp['moba_q_g'][0])
    put('mkg', inp['moba_k_g'][0])
    put('fqg', inp['fox_q_g'][0])
    put('fkg', inp['fox_k_g'][0])
    put('fbf', inp['fox_b_f'][0][hf * 8:(hf + 1) * 8])
    put('flag', np.full((P,), float(hf), np.float32))
    return vec


def core_inputs(inp, names, b, hf, cst):
    kh = slice(hf * 1024, (hf + 1) * 1024)
    hs = slice(hf * 1024, (hf + 1) * 1024)
    out = {}
    for nm in names:
        if nm == 'xT':
            out[nm] = np.ascontiguousarray(inp['x'][b, hf * TOK:(hf + 1) * TOK, :].T)
        elif nm == 'halo0':
            out[nm] = np.ascontiguousarray(inp['x'][b, TOK - HALO:TOK, :].T)
        elif nm == 'vecs':
            out[nm] = make_vecs(inp, b, hf)
        elif nm == 'cst':
            out[nm] = cst
        elif nm == 'mod_w_h':
            out[nm] = np.ascontiguousarray(inp['mod_w'][:, :, hf * 6144:(hf + 1) * 6144])
        elif nm.startswith('mlp_w1h_'):
            out[nm] = np.ascontiguousarray(inp['mlp_w1'][int(nm[-1]), kh, :])
        elif nm.startswith('mlp_w2h_'):
            out[nm] = np.ascontiguousarray(inp['mlp_w2'][int(nm[-1]), hf * 4096:(hf + 1) * 4096, :])
        elif nm.startswith('conv_w_in_h_'):
            out[nm] = np.ascontiguousarray(inp['conv_w_in'][int(nm[-1]), kh, :])
        elif nm.startswith('conv_w_out_h_'):
            out[nm] = np.ascontiguousarray(inp['conv_w_out'][int(nm[-1]), kh, :])
        elif nm == 'moba_wo_h':
            out[nm] = np.ascontiguousarray(inp['moba_w_o'][0, kh, :])
        elif nm == 'fox_wo_h':
            out[nm] = np.ascontiguousarray(inp['fox_w_o'][0, kh, :])
        elif nm == 'moba_qkv_l':
            w = inp['moba_w_qkv'][0]
            out[nm] = np.ascontiguousarray(np.concatenate([w[:, i * D:(i + 1) * D][:, hs] for i in range(3)], axis=1))
        elif nm == 'fox_w4_l':
            w = inp['fox_w_in'][0]
            out[nm] = np.ascontiguousarray(np.concatenate([w[:, i * D:(i + 1) * D][:, hs] for i in range(4)], axis=1))
        elif nm == 'fox_wf':
            out[nm] = np.ascontiguousarray(inp['fox_w_in'][0][:, 4 * D + hf * 8:4 * D + (hf + 1) * 8])
        else:
            raise KeyError(nm)
    return out


_PROG = {}


def run_cores(inp, cores, stop_after=99, dumps=(), raw=False, opts=()):
    key = (stop_after, len(cores), tuple(dumps), tuple(opts))
    if key not in _PROG:
        pairs = [[2 * i, 2 * i + 1] for i in range(len(cores) // 2)]
        bld = Builder(stop_after=stop_after, pairs=pairs, dumps=dumps, opts=opts)
        nc = bld.build()
        _PROG[key] = (nc, list(bld.inputs.keys()))
    nc, names = _PROG[key]
    cst = make_cst()
    in_maps = [core_inputs(inp, names, r // 2, r % 2, cst) for r in cores]
    res = run_bass_kernel_spmd(nc, in_maps, core_ids=list(range(len(cores))))
    if raw:
        return res.results
    return [r['outT'] for r in res.results]


def kernel(**inputs):
    inp = {k: np.asarray(v) for k, v in inputs.items()}
    outs = run_cores(inp, list(range(8)))
    B = inp['x'].shape[0]
    y = np.empty((B, SEQ, D), np.float32)
    for r, o in enumerate(outs):
        b, hf = r // 2, r % 2
        y[b, hf * TOK:(hf + 1) * TOK, :] = o.T
    return y
```

```python
import numpy as np
from contextlib import ExitStack
import concourse.bass as bass
import concourse.mybir as mybir
from concourse.bass_utils import run_bass_kernel_spmd

F32 = mybir.dt.float32
BF16 = mybir.dt.bfloat16
AF = mybir.ActivationFunctionType
ALU = mybir.AluOpType
AX = mybir.AxisListType

P = 128
D = 2048
DC = 16
FF = 8192
FC = 64
SEQ = 8192
TOK = 4096
T = 512
NT = 8
HALO = 32
NH = 8
EPS = 1e-6
CW = 31
NEGM = -30000.0
PAIRS = [[0, 1], [2, 3], [4, 5], [6, 7]]
LAYER_KIND = (0, 1, 2, 0)

ENG_ATTR = {'pe': 'tensor', 'act': 'scalar', 'dve': 'vector', 'pool': 'gpsimd', 'sp': 'sync'}
COMPUTE = ('pe', 'act', 'dve', 'pool')


class Res:
    __slots__ = ('name', 'w', 'wj', 'r', 'excl')

    def __init__(self, name='', excl=False):
        self.name = name
        self.w = {}
        self.wj = {}
        self.r = {}
        self.excl = excl


def PRes(name=''):
    return Res(name, excl=True)


class Sched:
    POOLS = {'dma': (28, 16), 'bg': (16, 16), 'cc': (6, 1)}

    def __init__(self, nc, stack):
        self.nc = nc
        self.ops = {e: [] for e in ENG_ATTR}
        self.cnt = {e: 0 for e in COMPUTE}
        self.known = {e: {} for e in ENG_ATTR}
        self.tot = {}
        self.rot = {p: 0 for p in self.POOLS}
        self.sems = {}
        for e in COMPUTE:
            self.sems[e] = stack.enter_context(nc.semaphore('prog_' + e))
        for p, (n, _) in self.POOLS.items():
            for i in range(n):
                self.sems[(p, i)] = stack.enter_context(nc.semaphore('%s%d' % (p, i)))
                self.tot[(p, i)] = 0
        self.nblocks = 0

    def _collect(self, eng, reads, writes, joins, extra=None):
        need = {}

        def add(k, v):
            if need.get(k, 0) < v:
                need[k] = v
        for r in reads:
            for k, v in r.w.items():
                add(k, v)
            for k, v in r.wj.items():
                add(k, v)
            if r.excl:
                for k, v in r.r.items():
                    if k != eng:
                        add(k, v)
        for w in writes:
            for k, v in w.w.items():
                add(k, v)
            for k, v in w.wj.items():
                add(k, v)
            for k, v in w.r.items():
                add(k, v)
        for w in joins:
            for k, v in w.w.items():
                add(k, v)
            for k, v in w.r.items():
                add(k, v)
        if extra:
            add(*extra)
        known = self.known[eng]
        waits = []
        for k, v in need.items():
            if known.get(k, 0) < v:
                known[k] = v
                waits.append((k, v))
        return waits

    def _commit(self, ev, reads, writes, joins):
        k, v = ev
        for r in reads:
            if r.r.get(k, 0) < v:
                r.r[k] = v
        for w in writes:
            w.w = {k: v}
            w.wj = {}
            w.r = {}
        for w in joins:
            if w.wj.get(k, 0) < v:
                w.wj[k] = v

    def op(self, eng, fn, reads=(), writes=(), joins=()):
        waits = self._collect(eng, reads, writes, joins)
        self.cnt[eng] += 1
        ev = (eng, self.cnt[eng])
        self.ops[eng].append((waits, fn, ev, 1))
        self._commit(ev, reads, writes, joins)

    def aop(self, eng, fn, pool='dma', reads=(), writes=(), joins=()):
        n, inc = self.POOLS[pool]
        s = self.rot[pool]
        self.rot[pool] = (s + 1) % n
        key = (pool, s)
        prev = (key, self.tot[key]) if self.tot[key] else None
        waits = self._collect(eng, reads, writes, joins, extra=prev)
        self.tot[key] += inc
        ev = (key, self.tot[key])
        self.ops[eng].append((waits, fn, ev, inc))
        self._commit(ev, reads, writes, joins)

    def flush(self, name, drain=('dma',)):
        nc = self.nc
        final = []
        for key, t in self.tot.items():
            if t and key[0] in drain and self.known['sp'].get(key, 0) < t:
                self.known['sp'][key] = t
                final.append((key, t))
        sems = self.sems
        oplists = self.ops
        self.ops = {e: [] for e in ENG_ATTR}
        self.nblocks += 1
        with nc.Block() as block:
            def make(ename):
                oplist = oplists[ename]

                def body(engine):
                    for waits, fn, ev, inc in oplist:
                        for k, v in waits:
                            engine.wait_ge(sems[k], v)
                        ins = fn(engine)
                        ins.then_inc(sems[ev[0]], inc)
                    if ename == 'sp':
                        for k, v in final:
                            engine.wait_ge(sems[k], v)
                return body
            for ename, attr in ENG_ATTR.items():
                getattr(block, attr)(make(ename))


def vec_layout():
    off = {}
    n = 0

    def add(name, w):
        nonlocal n
        off[name] = n
        n += w
    add('c', 16)
    for l in range(4):
        add('n1g%d' % l, 16)
        add('n2g%d' % l, 16)
    add('modb', 192)
    for cj in range(2):
        add('cbia%d' % cj, 16)
        add('cbig%d' % cj, 16)
        add('cdww%d' % cj, 16 * CW)
        add('cdwb%d' % cj, 16)
        add('clng%d' % cj, 16)
        add('clnb%d' % cj, 16)
        add('cbo%d' % cj, 16)
    for nm in ('mqg', 'mkg', 'fqg', 'fkg', 'fbf', 'flag'):
        add(nm, 1)
    return off, n


VOFF, NV = vec_layout()
CST_IDENT, CST_TRI, CST_ONES, CST_E, CST_EF = 0, 128, 256, 384, 384 + 32 * 128
NCST = CST_EF + 8 * 128

FULLW = {}
for _l in range(4):
    FULLW['w1_%d' % _l] = (D, FF, 'mlp_w1h_%d' % _l, None)
    FULLW['w2_%d' % _l] = (FF, D, 'mlp_w2h_%d' % _l, None)
for _c in range(2):
    FULLW['cin_%d' % _c] = (D, 2 * D, 'conv_w_in_h_%d' % _c, None)
    FULLW['cout_%d' % _c] = (D, D, 'conv_w_out_h_%d' % _c, None)
FULLW['mwo'] = (D, D, 'moba_wo_h', None)
FULLW['fwo'] = (D, D, 'fox_wo_h', None)
BUNDLES = [['cin_0', 'cout_0'], ['w1_0'], ['w2_0'], ['mwo'], ['w1_1'], ['w2_1'], ['fwo'],
           ['w1_2'], ['w2_2'], ['cin_1', 'cout_1'], ['w1_3'], ['w2_3']]


def wgeom(K, N):
    nw = 512 if K == D else 128
    return nw, N // nw, (K // P) // 2


class Builder:
    def __init__(self, stop_after=99, pairs=None, dumps=(), opts=()):
        self.opts = set(opts)
        self.stop_after = stop_after
        self.pairs = pairs or PAIRS
        self.dumps = []
        self.dump_names = dumps
        self.nc = bass.Bass("TRN2", target_bir_lowering=False)
        self.inputs = {}
        self.st = ExitStack()
        self.S = None

    def inp(self, name, shape, dtype=F32):
        if name not in self.inputs:
            t = self.nc.dram_tensor(name, list(shape), dtype, kind="ExternalInput")
            self.inputs[name] = (t, tuple(shape))
        return self.inputs[name][0]

    def dram(self, name, shape, dtype):
        return self.nc.dram_tensor(name, list(shape), dtype)

    def _uid(self, name):
        self._n = getattr(self, '_n', 0) + 1
        return '%s_%d' % (name, self._n)

    def sb(self, stack, name, shape, dtype):
        return stack.enter_context(self.nc.sbuf_tensor(self._uid(name), list(shape), dtype))

    def ps(self, stack, name):
        return stack.enter_context(self.nc.psum_tensor(self._uid(name), [P, 512], F32))

    def setup_weights(self):
        self.wch = {}
        self.wdone = set()
        for nm, (K, N, _, _) in FULLW.items():
            nw, ng, kh = wgeom(K, N)
            lst = []
            for j in range(ng // 2):
                lst.append(dict(half=self.dram('wh_%s_%d' % (nm, j), [256, 4096], BF16),
                                full=self.dram('wf_%s_%d' % (nm, j), [512, 4096], BF16),
                                Rh=Res(), Rf=Res()))
            self.wch[nm] = lst
        self.lrows = (6 + 8) * P
        self.lfull = self.dram('wlf', [2 * self.lrows, 4096], BF16)
        self.Rl = Res('wlf')
        self.lloc = {'mqkv': 0, 'fw4': 6 * P}

    def cast_group(self, dst2d, row0, kh, nw, src_ap_rows, segs, join):
        S = self.S
        dst = dst2d[row0:row0 + P, :].rearrange("p (kk n) -> p kk n", n=nw)
        o = 0
        for (c0, w) in segs:
            src = src_ap_rows[:, c0:c0 + w].rearrange("(kk p) n -> p kk n", p=P)
            d = dst[:, :, o:o + w]
            S.aop('pool', lambda e, d=d, src=src: e.dma_start(out=d, in_=src), pool='bg', joins=[join])
            o += w

    def allgather(self, src_t, dst_t, Rsrc, Rdst):
        self.S.aop('pool', lambda e: e.collective_compute("AllGather", ALU.bypass, replica_groups=self.pairs,
                                                          ins=[src_t.ap().opt()], outs=[dst_t.ap().opt()]),
                   pool='cc', reads=[Rsrc], writes=[Rdst])

    def prep_bundle(self, bi):
        for nm in BUNDLES[bi]:
            if nm in self.wdone:
                continue
            self.wdone.add(nm)
            K, N, iname, idx = FULLW[nm]
            nw, ng, kh = wgeom(K, N)
            t = self.inp(iname, [K // 2, N])
            src = t.ap()
            for g in range(ng):
                ch = self.wch[nm][g // 2]
                if nm.startswith('cin'):
                    segs = [(g * 256, 256), (D + g * 256, 256)]
                else:
                    segs = [(g * nw, nw)]
                self.cast_group(ch['half'].ap(), (g % 2) * P, kh, nw, src, segs, ch['Rh'])
                if g % 2 == 1:
                    self.allgather(ch['half'], ch['full'], ch['Rh'], ch['Rf'])

    def prep_local(self):
        S = self.S
        lf = self.lfull.ap()
        for nm, iname, ncols, r0 in (('mqkv', 'moba_qkv_l', 3072, 0), ('fw4', 'fox_w4_l', 4096, 6 * P)):
            t = self.inp(iname, [D, ncols])
            for khalf in range(2):
                src = t.ap()[khalf * 1024:(khalf + 1) * 1024, :]
                for g in range(ncols // 512):
                    self.cast_group(lf, khalf * self.lrows + r0 + g * P, 8, 512, src, [(g * 512, 512)], self.Rl)

    def slab_src(self, wname, g):
        if wname in self.lloc:
            r0 = self.lloc[wname] + g * P
            v = self.lfull.ap().rearrange("(k r) c -> r k c", k=2)
            return v[r0:r0 + P], self.Rl
        ch = self.wch[wname][g // 2]
        v = ch['full'].ap().rearrange("(k r) c -> r k c", k=2)
        return v[(g % 2) * P:(g % 2 + 1) * P], ch['Rf']

    def load_slab(self, wname, g):
        S = self.S
        i = self.slab_i
        self.slab_i = (i + 1) % len(self.slabs)
        buf, R = self.slabs[i], self.Rslab[i]
        src, Rsrc = self.slab_src(wname, g)
        S.aop('sp', lambda e: e.dma_start(out=buf[:], in_=src), reads=[Rsrc], writes=[R])
        return buf, R

    def next_mm(self):
        i = self.mm_i
        self.mm_i = (i + 1) % len(self.mmb)
        return self.mmb[i], self.Rmm[i]

    def norm_adaln(self, xt, Rxt, Tn, A, B, h, Rh):
        S = self.S
        sq, Rsq = self.sq, self.Ru
        pst, Rpst = self.pst, self.Rpst
        ones = self.ones_bf
        S.op('act', lambda e: e.activation(out=sq[:, :, 0:Tn], in_=xt, func=AF.Square), reads=[Rxt], writes=[Rsq])

        def g(e):
            for c in range(DC):
                ins = e.matmul(pst[:, 0:Tn], lhsT=ones[:], rhs=sq[:, c, 0:Tn], start=(c == 0), stop=(c == DC - 1))
            return ins
        S.op('pe', g, reads=[Rsq, self.Rcst], writes=[Rpst])
        rs, Rrs = self.tmp('rs')
        S.op('act', lambda e: e.activation(out=rs[:, 0:Tn], in_=pst[:, 0:Tn], func=AF.Ln, scale=1.0 / D, bias=self.eps_ap),
             reads=[Rpst, self.Rcst], writes=[Rrs])
        rstd, Rrstd = self.tmp('rstd')
        S.op('act', lambda e: e.activation(out=rstd[:, 0:Tn], in_=rs[:, 0:Tn], func=AF.Exp, scale=-0.5), reads=[Rrs], writes=[Rrstd])
        for c in range(DC):
            tm, Rtm = self.tmp('nt%d' % (c % 2))
            S.op('dve', lambda e, c=c, tm=tm: e.tensor_tensor(out=tm[:, 0:Tn], in0=xt[:, c, :], in1=rstd[:, 0:Tn], op=ALU.mult),
                 reads=[Rxt, Rrstd], writes=[Rtm])
            S.op('act', lambda e, c=c, tm=tm: e.activation(out=h[:, c, 0:Tn], in_=tm[:, 0:Tn], func=AF.Identity,
                                                           scale=A[:, c:c + 1], bias=B[:, c:c + 1]),
                 reads=[Rtm, self.Rcoef], joins=[Rh])

    def tmp(self, name):
        return self.tmps[name], self.Rtmps[name]

    def mm_group(self, ps, Rps, slab, Rslab, kchunks, lhs_fn, rhs_fn, reads):
        def g(e):
            n = len(kchunks)
            for i, kc in enumerate(kchunks):
                ins = e.matmul(ps, lhsT=lhs_fn(kc), rhs=rhs_fn(kc), start=(i == 0), stop=(i == n - 1))
            return ins
        self.S.op('pe', g, reads=[Rslab] + list(reads), writes=[Rps])

    def mlp(self, l, xt, Rxt, store_fn=None):
        S = self.S
        h, Rh, u, Ru = self.h, self.Rh, self.u, self.Ru
        cf = self.coef
        self.norm_adaln(xt[:, :, :], Rxt, T, cf[:, l, 4, :], cf[:, l, 3, :], h, Rh)
        for g in range(16):
            slab, Rs = self.load_slab('w1_%d' % l, g)
            for mm in range(4):
                m = g * 4 + mm
                ps, Rps = self.next_mm()
                self.mm_group(ps[:, :], Rps, slab, Rs, range(DC),
                              lambda kc, mm=mm, slab=slab: slab[:, kc // 8, (kc % 8) * 512 + mm * P:(kc % 8) * 512 + (mm + 1) * P],
                              lambda kc: h[:, kc, :], [Rh])
                sqt, Rsqt = self.tmp('sq%d' % (m % 2))
                S.op('act', lambda e, ps=ps, sqt=sqt: e.activation(out=sqt[:, :], in_=ps[:, :], func=AF.Square),
                     reads=[Rps], writes=[Rsqt])
                S.op('dve', lambda e, ps=ps, sqt=sqt, m=m: e.scalar_tensor_tensor(
                    out=u[:, m, :], in0=ps[:, :], scalar=0.0, in1=sqt[:, :], op0=ALU.is_gt, op1=ALU.mult),
                    reads=[Rps, Rsqt], joins=[Ru])
        for g in range(16):
            slab, Rs = self.load_slab('w2_%d' % l, g)
            ps, Rps = self.next_mm()
            self.mm_group(ps[:, :], Rps, slab, Rs, range(FC),
                          lambda kc, slab=slab: slab[:, kc // 32, (kc % 32) * P:(kc % 32 + 1) * P],
                          lambda kc: u[:, kc, :], [Ru])
            S.op('dve', lambda e, ps=ps, g=g: e.scalar_tensor_tensor(
                out=xt[:, g, :], in0=ps[:, :], scalar=cf[:, l, 5, g:g + 1], in1=xt[:, g, :], op0=ALU.mult, op1=ALU.add),
                reads=[Rps, self.Rcoef, Rxt], joins=[Rxt])
            if store_fn is not None:
                store_fn(g)

    def conv_glu(self, cj, Tn, dst_fn):
        S = self.S
        h, Rh = self.h, self.Rh
        vec = self.vec
        oa, og = VOFF['cbia%d' % cj], VOFF['cbig%d' % cj]
        for g in range(8):
            slab, Rs = self.load_slab('cin_%d' % cj, g)
            for cc in range(2):
                c = 2 * g + cc
                psa, Rpa = self.next_mm()
                psg, Rpg = self.next_mm()
                for (ps, Rps, col) in ((psa, Rpa, cc), (psg, Rpg, 2 + cc)):
                    self.mm_group(ps[:, 0:Tn], Rps, slab, Rs, range(DC),
                                  lambda kc, col=col, slab=slab: slab[:, kc // 8, (kc % 8) * 512 + col * P:(kc % 8) * 512 + (col + 1) * P],
                                  lambda kc: h[:, kc, 0:Tn], [Rh])
                sg, Rsg = self.tmp('sg%d' % (c % 2))
                S.op('act', lambda e, psg=psg, sg=sg, c=c: e.activation(out=sg[:, 0:Tn], in_=psg[:, 0:Tn], func=AF.Sigmoid,
                                                                       bias=vec[:, og + c:og + c + 1], scale=1.0),
                     reads=[Rpg, self.Rvec], writes=[Rsg])
                dst, Rd, kind = dst_fn(c)
                S.op('dve', lambda e, psa=psa, sg=sg, c=c, dst=dst: e.scalar_tensor_tensor(
                    out=dst, in0=psa[:, 0:Tn], scalar=vec[:, oa + c:oa + c + 1], in1=sg[:, 0:Tn], op0=ALU.add, op1=ALU.mult),
                    reads=[Rpa, Rsg, self.Rvec], **{kind: [Rd]})
                yield c

    def conv_halo(self, l, cj, halo_src, Rsrc):
        S = self.S
        xh, Rxh = self.tmps['xh'], self.Rtmps['xh']
        S.aop('sp', lambda e: e.dma_start(out=xh[:, :, :], in_=halo_src), reads=[Rsrc], writes=[Rxh])
        cf = self.coef
        self.norm_adaln(xh[:, :, :], Rxh, HALO, cf[:, l, 1, :], cf[:, l, 0, :], self.h, self.Rh)
        carry, Rc = self.carry, self.Rcarry
        fo = VOFF['flag']
        for c in self.conv_glu(cj, HALO, lambda c: (self.carry[:, c, :], self.Rcarry, 'joins')):
            pass
        S.op('dve', lambda e: e.tensor_scalar(out=carry[:, :, :], in0=carry[:, :, :], scalar1=self.vec[:, fo:fo + 1], scalar2=None,
                                             op0=ALU.mult), reads=[self.Rvec], writes=[Rc])

    def conv_mixer(self, l, cj, xt, Rxt):
        S = self.S
        vec, cf = self.vec, self.coef
        h, Rh, u, Ru = self.h, self.Rh, self.u, self.Ru
        carry, Rc = self.carry, self.Rcarry
        v = self.v
        vb, v2b = self.vb, self.v2b
        self.norm_adaln(xt[:, :, :], Rxt, T, cf[:, l, 1, :], cf[:, l, 0, :], h, Rh)
        cbo = self.cb
        for c in range(DC):
            S.op('act', lambda e, c=c: e.activation(out=xt[:, c, :], in_=xt[:, c, :], func=AF.Identity,
                                                    bias=cbo[:, cj, c:c + 1], scale=1.0),
                 reads=[self.Rcoef, Rxt], joins=[Rxt])
        ow, ob = VOFF['cdww%d' % cj], VOFF['cdwb%d' % cj]
        ucs = {}

        def dst_fn(c):
            uc, Ruc = self.tmp('uc%d' % (c % 2))
            ucs[c] = (uc, Ruc)
            return uc[:, HALO:HALO + T], Ruc, 'writes'
        def emit_fir(c):
            uc, Ruc = ucs[c]
            S.op('act', lambda e, uc=uc, c=c: e.copy(out=uc[:, 0:HALO], in_=carry[:, c, :]), reads=[Rc], joins=[Ruc])
            fir, Rfir = self.firb[c % 2], self.Rfir[c % 2]
            dg, Rdg = self.dgb[c % 2], self.Rdg[c % 2]
            wsl = vec[:, ow + c * CW:ow + (c + 1) * CW]
            S.op('dve', lambda e, dg=dg, wsl=wsl: e.tensor_tensor(
                out=dg[:, :, :], in0=self.ident_bf[:, :].unsqueeze(1).broadcast_to([P, CW, P]),
                in1=wsl.unsqueeze(2).broadcast_to([P, CW, P]), op=ALU.mult),
                reads=[self.Rvec, self.Rcst], writes=[Rdg])

            def gfir(e, uc=uc, fir=fir, dg=dg):
                for j in range(CW):
                    ins = e.matmul(fir[:, :], lhsT=dg[:, j, :], rhs=uc[:, 2 + j:2 + j + T], start=(j == 0), stop=(j == CW - 1))
                return ins
            S.op('pe', gfir, reads=[Ruc, Rdg], writes=[Rfir])
            S.op('act', lambda e, fir=fir, c=c: e.activation(out=v[:, c, :], in_=fir[:, :], func=AF.Identity,
                                                            bias=vec[:, ob + c:ob + c + 1], scale=1.0),
                 reads=[Rfir, self.Rvec], joins=[Ru])
            S.op('act', lambda e, uc=uc, c=c: e.copy(out=carry[:, c, :], in_=uc[:, T:T + HALO]), reads=[Ruc], joins=[Rc])
        prev = None
        for c in self.conv_glu(cj, T, dst_fn):
            if prev is not None:
                emit_fir(prev)
            prev = c
        emit_fir(prev)
        Rvb = self.Rvb
        S.op('act', lambda e: e.copy(out=vb[:, :, :], in_=v[:, :, :]), reads=[Ru], joins=[Rvb])
        S.op('act', lambda e: e.activation(out=v2b[:, :, :], in_=v[:, :, :], func=AF.Square), reads=[Ru], joins=[Rvb])
        ones = self.ones_bf
        ps1, Rp1 = self.pst, self.Rpst
        ps2, Rp2 = self.pst2, self.Rpst2
        for (ps, Rps, src) in ((ps1, Rp1, vb), (ps2, Rp2, v2b)):
            def g(e, ps=ps, src=src):
                for c in range(DC):
                    ins = e.matmul(ps[:, :], lhsT=ones[:], rhs=src[:, c, :], start=(c == 0), stop=(c == DC - 1))
                return ins
            S.op('pe', g, reads=[Rvb, self.Rcst], writes=[Rps])
        mean, Rmean = self.tmp('rs')
        S.op('act', lambda e: e.activation(out=mean[:, :], in_=ps1[:, :], func=AF.Identity, scale=1.0 / D), reads=[Rp1], writes=[Rmean])
        m2, Rm2 = self.tmp('nt0')
        S.op('dve', lambda e: e.tensor_tensor(out=m2[:, :], in0=mean[:, :], in1=mean[:, :], op=ALU.mult), reads=[Rmean], writes=[Rm2])
        var, Rvar = self.tmp('nt1')
        S.op('dve', lambda e: e.scalar_tensor_tensor(out=var[:, :], in0=ps2[:, :], scalar=1.0 / D, in1=m2[:, :],
                                                     op0=ALU.mult, op1=ALU.subtract), reads=[Rp2, Rm2], writes=[Rvar])
        sd, Rsd = self.tmp('sg0')
        S.op('act', lambda e: e.activation(out=sd[:, :], in_=var[:, :], func=AF.Ln, bias=self.eps_ap, scale=1.0),
             reads=[Rvar, self.Rcst], writes=[Rsd])
        rstd, Rrstd = self.tmp('rstd')
        S.op('act', lambda e: e.activation(out=rstd[:, :], in_=sd[:, :], func=AF.Exp, scale=-0.5), reads=[Rsd], writes=[Rrstd])
        nmr, Rnmr = self.tmp('sg1')
        S.op('dve', lambda e: e.scalar_tensor_tensor(out=nmr[:, :], in0=mean[:, :], scalar=-1.0, in1=rstd[:, :],
                                                     op0=ALU.mult, op1=ALU.mult), reads=[Rmean, Rrstd], writes=[Rnmr])
        og, obb = VOFF['clng%d' % cj], VOFF['clnb%d' % cj]
        for c in range(DC):
            t1, Rt1 = self.tmp('sq%d' % (c % 2))
            S.op('dve', lambda e, c=c, t1=t1: e.tensor_tensor(out=t1[:, :], in0=v[:, c, :], in1=rstd[:, :], op=ALU.mult),
                 reads=[Ru, Rrstd], writes=[Rt1])
            S.op('dve', lambda e, t1=t1: e.tensor_tensor(out=t1[:, :], in0=t1[:, :], in1=nmr[:, :], op=ALU.add),
                 reads=[Rnmr], writes=[Rt1])
            S.op('act', lambda e, c=c, t1=t1: e.activation(out=h[:, c, :], in_=t1[:, :], func=AF.Silu,
                                                           scale=vec[:, og + c:og + c + 1], bias=vec[:, obb + c:obb + c + 1]),
                 reads=[Rt1, self.Rvec], writes=[Rh] if c == 0 else (), joins=[Rh] if c else ())
        for g in range(4):
            slab, Rs = self.load_slab('cout_%d' % cj, g)
            for mm in range(4):
                m = g * 4 + mm
                ps, Rps = self.next_mm()
                self.mm_group(ps[:, :], Rps, slab, Rs, range(DC),
                              lambda kc, mm=mm, slab=slab: slab[:, kc // 8, (kc % 8) * 512 + mm * P:(kc % 8) * 512 + (mm + 1) * P],
                              lambda kc: h[:, kc, :], [Rh])
                S.op('dve', lambda e, ps=ps, m=m: e.scalar_tensor_tensor(
                    out=xt[:, m, :], in0=ps[:, :], scalar=cf[:, l, 2, m:m + 1], in1=xt[:, m, :], op0=ALU.mult, op1=ALU.add),
                    reads=[Rps, self.Rcoef, Rxt], joins=[Rxt])

    def phase_prep(self):
        nc, S = self.nc, self.S
        with ExitStack() as ph:
            self.prep_bundle(0)
            self.prep_bundle(1)
            self.prep_bundle(2)
            vec, Rvec = self.vec, self.Rvec
            vin = self.inp('vecs', [P, NV])
            S.aop('sp', lambda e: e.dma_start(out=vec[:, :], in_=vin.ap()), writes=[Rvec])
            cin = self.inp('cst', [P, NCST])
            ctmp = self.sb(ph, 'ctmp', [P, 384], F32)
            Rct = Res('ctmp')
            S.aop('sp', lambda e: e.dma_start(out=ctmp[:, :], in_=cin.ap()[:, 0:384]), writes=[Rct])
            S.op('dve', lambda e: e.tensor_copy(out=self.ident[:, :], in_=ctmp[:, 0:128]), reads=[Rct], joins=[self.Rcst])
            S.op('dve', lambda e: e.tensor_copy(out=self.tri_bf[:, :], in_=ctmp[:, 128:256]), reads=[Rct], joins=[self.Rcst])
            S.op('dve', lambda e: e.tensor_copy(out=self.ones_bf[:, :], in_=ctmp[:, 256:384]), reads=[Rct], joins=[self.Rcst])
            S.op('dve', lambda e: e.memset(self.eps_ap, EPS), joins=[self.Rcst])
            S.op('dve', lambda e: e.tensor_copy(out=self.ident_bf[:, :], in_=ctmp[:, 0:128]), reads=[Rct], joins=[self.Rcst])
            S.op('dve', lambda e: e.tensor_scalar(out=self.trineg[:, :], in0=ctmp[:, 128:256], scalar1=-1.0, scalar2=-NEGM, op0=ALU.add, op1=ALU.mult),
                 reads=[Rct], joins=[self.Rcst])
            S.op('dve', lambda e: e.memset(self.one_ap, 1.0), joins=[self.Rcst])
            fo_ = VOFF['flag']
            S.op('dve', lambda e: e.tensor_scalar(out=self.nflag[:, :], in0=vec[:, fo_:fo_ + 1], scalar1=-1.0, scalar2=1.0, op0=ALU.mult, op1=ALU.add),
                 reads=[Rvec], joins=[self.Rcoef])
            ob_ = VOFF['modb']
            cond = self.sb(ph, 'cond', [P, 16], F32)
            Rcond = Res('cond')
            oc = VOFF['c']
            S.op('act', lambda e: e.activation(out=cond[:, :], in_=vec[:, oc:oc + 16], func=AF.Silu), reads=[Rvec], writes=[Rcond])
            nomod = 'nomod' in self.opts
            mw = None if nomod else self.inp('mod_w_h', [4, D, 6144])
            wb = [self.sb(ph, 'mwb%d' % i, [P, 6144], F32) for i in range(2)]
            Rwb = [Res('mwb%d' % i) for i in range(2)]
            psm = self.ps(ph, 'psm')
            Rpsm = PRes('psm')
            it = 0
            if nomod:
                S.op('pe', lambda e: e.matmul(psm[:, 0:192], lhsT=self.ident[:, :], rhs=vec[:, ob_:ob_ + 192], start=True, stop=True),
                     reads=[Rvec, self.Rcst], writes=[Rpsm])
            for l in range(0 if nomod else 4):
                for k in range(DC):
                    b, Rb = wb[it % 2], Rwb[it % 2]
                    S.aop('sp', lambda e, b=b, l=l, k=k: e.dma_start(out=b[:, :], in_=mw.ap()[l, k * P:(k + 1) * P, :]), writes=[Rb])

                    def g(e, b=b, l=l, k=k, it=it):
                        for m in range(48):
                            ins = e.matmul(psm[:, l * 48 + m:l * 48 + m + 1], lhsT=b[:, m * P:(m + 1) * P], rhs=cond[:, k:k + 1],
                                           start=(it == 0 and m == 0), stop=(it == 63 and m == 47), skip_group_check=True)
                        return ins
                    S.op('pe', g, reads=[Rb, Rcond], joins=[Rpsm])
                    it += 1
            modh = self.sb(ph, 'modh', [P, 192], F32)
            Rmodh = Res('modh')
            ob = VOFF['modb']
            S.op('dve', lambda e: e.tensor_tensor(out=modh[:, :], in0=psm[:, 0:192], in1=vec[:, ob:ob + 192], op=ALU.add),
                 reads=[Rpsm, Rvec], writes=[Rmodh])
            msend = self.dram('modsend', [P, 192], F32)
            mfull = self.dram('modfull', [2 * P, 192], F32)
            Rms, Rmf = Res('ms'), Res('mf')
            S.aop('sp', lambda e: e.dma_start(out=msend.ap(), in_=modh[:, :]), reads=[Rmodh], writes=[Rms])
            self.allgather(msend, mfull, Rms, Rmf)
            modt = self.sb(ph, 'modt', [P, 2, 192], F32)
            Rmodt = Res('modt')
            S.aop('sp', lambda e: e.dma_start(out=modt[:, :, :], in_=mfull.ap().rearrange("(k p) c -> p k c", k=2)),
                  reads=[Rmf], writes=[Rmodt])
            cf, Rcf = self.coef, self.Rcoef
            for l in range(4):
                for hf_, (jb, ja, jg, gname) in enumerate(((0, 1, 2, 'n1g%d'), (3, 4, 5, 'n2g%d'))):
                    base = l * 48
                    go = VOFF[gname % l]
                    S.op('dve', lambda e, l=l, hf_=hf_, jb=jb, base=base: e.tensor_copy(out=cf[:, l, jb, :], in_=modt[:, hf_, base:base + 16]),
                         reads=[Rmodt], joins=[Rcf])
                    S.op('dve', lambda e, l=l, hf_=hf_, ja=ja, base=base, go=go: e.scalar_tensor_tensor(
                        out=cf[:, l, ja, :], in0=modt[:, hf_, base + 16:base + 32], scalar=1.0, in1=vec[:, go:go + 16],
                        op0=ALU.add, op1=ALU.mult), reads=[Rmodt, Rvec], joins=[Rcf])
                    S.op('dve', lambda e, l=l, hf_=hf_, jg=jg, base=base: e.tensor_scalar(
                        out=cf[:, l, jg, :], in0=modt[:, hf_, base + 32:base + 48], scalar1=1.0, scalar2=None, op0=ALU.add),
                        reads=[Rmodt], joins=[Rcf])
            for cj, l in ((0, 0), (1, 3)):
                bo = VOFF['cbo%d' % cj]
                S.op('dve', lambda e, cj=cj, l=l, bo=bo: e.tensor_tensor(out=self.cb[:, cj, :], in0=cf[:, l, 2, :], in1=vec[:, bo:bo + 16], op=ALU.mult),
                     reads=[Rcf, Rvec], joins=[Rcf])
            S.flush('prep')

    def alloc_tok(self, ph, nslab=3):
        self.xt = self.sb(ph, 'xt', [P, DC, T], F32)
        self.Rxt = Res('xt')
        self.h = self.sb(ph, 'h', [P, DC, T], BF16)
        self.Rh = Res('h')
        self.u = self.sb(ph, 'u', [P, FC, T], BF16)
        self.Ru = Res('u')
        self.sq = self.u[:, 0:DC, :]
        self.slabs = [self.sb(ph, 'slab%d' % i, [P, 2, 4096], BF16) for i in range(nslab)]
        self.Rslab = [Res('slab%d' % i) for i in range(nslab)]
        self.slab_i = 0
        self.tmps, self.Rtmps = {}, {}
        for nm in ('rs', 'rstd', 'nt0', 'nt1', 'sq0', 'sq1', 'sg0', 'sg1'):
            self.tmps[nm] = self.sb(ph, 't_' + nm, [P, T], F32)
            self.Rtmps[nm] = Res(nm)
        self.mmb = [self.ps(ph, 'mm%d' % i) for i in range(4)]
        self.Rmm = [PRes('mm%d' % i) for i in range(4)]
        self.mm_i = 0
        self.pst = self.ps(ph, 'pst')
        self.Rpst = PRes('pst')

    def alloc_conv(self, ph):
        nc = self.nc
        for nm in ('uc0', 'uc1'):
            self.tmps[nm] = self.sb(ph, 't_' + nm, [P, HALO + T], BF16)
            self.Rtmps[nm] = Res(nm)
        self.dgb = [self.sb(ph, 'dg%d' % i, [P, CW, P], BF16) for i in range(2)]
        self.Rdg = [Res() for i in range(2)]
        self.tmps['xh'] = self.sb(ph, 't_xh', [P, DC, HALO], F32)
        self.Rtmps['xh'] = Res('xh')
        self.carry = self.sb(ph, 'carry', [P, DC, HALO], BF16)
        self.Rcarry = Res('carry')
        self.firb = [self.ps(ph, 'fir%d' % i) for i in range(2)]
        self.Rfir = [PRes('fir%d' % i) for i in range(2)]
        self.pst2 = self.ps(ph, 'pst2')
        self.Rpst2 = PRes('pst2')
        uu = self.u
        flat = uu[:, :, :].rearrange("p a b -> p (a b)")
        self.v = flat[:, 0:2 * DC * T].bitcast(F32).rearrange("p (a b) -> p a b", b=T)
        self.vb = uu[:, 32:48, :]
        self.v2b = uu[:, 48:64, :]
        self.Rvb = self.Ru

    def tile_io(self, src2d, i, n=T, c0=None):
        c0 = i * T if c0 is None else c0
        return src2d.rearrange("(c p) t -> p c t", p=P)[:, :, c0:c0 + n]

    def phase_conv_layer(self, l, cj, x_src, Rsrc, halo_src, Rhalo, x_dst, Rdst, emit_next):
        S = self.S
        with ExitStack() as ph:
            self.alloc_tok(ph)
            self.alloc_conv(ph)
            xt, Rxt = self.xt, self.Rxt
            self.conv_halo(l, cj, halo_src, Rhalo)
            for i in range(NT):
                S.aop('sp', lambda e, i=i: e.dma_start(out=xt[:, :, :], in_=self.tile_io(x_src, i)), reads=[Rsrc], writes=[Rxt])
                self.conv_mixer(l, cj, xt, Rxt)
                self.mlp(l, xt, Rxt)
                S.aop('sp', lambda e, i=i: e.dma_start(out=self.tile_io(x_dst, i), in_=xt[:, :, :]), reads=[Rxt], joins=[Rdst])
                if emit_next is not None:
                    emit_next(i, xt, Rxt)
            S.flush('conv%d' % l)

    def emit_hsend(self, lnext):
        cf = self.coef

        def f(i, xt, Rxt):
            S = self.S
            self.norm_adaln(xt[:, :, :], Rxt, T, cf[:, lnext, 1, :], cf[:, lnext, 0, :], self.h, self.Rh)
            hs = self.hsend[i]
            S.aop('sp', lambda e: e.dma_start(out=hs.ap().rearrange("(c p) t -> p c t", p=P), in_=self.h[:, :, :]), reads=[self.Rh], writes=[self.Rhsend[i]])
            self.allgather(hs, self.hfull[i], self.Rhsend[i], self.Rhfull[i])
        return f

    def phase_proj(self, kind):
        nc, S = self.nc, self.S
        moba = (kind == 'moba')
        wname = 'mqkv' if moba else 'fw4'
        nsl = 6 if moba else 8
        vec = self.vec
        qg_o, kg_o = (VOFF['mqg'], VOFF['mkg']) if moba else (VOFF['fqg'], VOFF['fkg'])
        SCALE = float(P) ** -0.5
        with ExitStack() as ph:
            wres = [self.sb(ph, 'wres%d' % i, [P, 2, 4096], BF16) for i in range(nsl)]
            Rw = Res('wres')
            for i in range(nsl):
                src, Rsrc = self.slab_src(wname, i)
                S.aop('sp', lambda e, i=i, src=src: e.dma_start(out=wres[i][:], in_=src), reads=[Rsrc], joins=[Rw])
            nhb = 2 if moba else 1
            hb = [self.sb(ph, 'hb%d' % i, [P, DC, T], BF16) for i in range(nhb)]
            Rhb = [Res() for i in range(nhb)]
            tm = {}
            Rt = {}
            for nm, dt in (('raw0', F32), ('raw1', F32), ('rs0', F32), ('rs1', F32), ('rstd0', F32), ('rstd1', F32),
                           ('kn0', F32), ('kn1', F32), ('sqb0', BF16), ('sqb1', BF16), ('ob0', BF16), ('ob1', BF16),
                           ('vb0', BF16), ('vb1', BF16)):
                tm[nm] = self.sb(ph, 'p_' + nm, [P, T], dt)
                Rt[nm] = Res(nm)
            mmb = [self.ps(ph, 'mm%d' % i) for i in range(4)]
            Rmm = [PRes() for i in range(4)]
            stb = [self.ps(ph, 'st%d' % i) for i in range(2)]
            Rst = [PRes() for i in range(2)]
            psx = self.ps(ph, 'psx')
            Rpsx = PRes('psx')
            pstr = self.ps(ph, 'pstr')
            Rpstr = PRes('pstr')
            mmi = [0]

            def next_mm():
                i = mmi[0]
                mmi[0] = (i + 1) % 4
                return mmb[i], Rmm[i]
            ones = self.ones_bf
            if moba:
                kmean = self.sb(ph, 'kmean', [P, NH, 32], F32)
                Rkm = Res('kmean')
                S.op('dve', lambda e: e.memset(kmean[:, :, :], 0.0), writes=[Rkm])
                gsb4 = self.sb(ph, 'gsb4', [P, 4, 32], F32)
                Rgsb = Res('gsb')
                mx84 = self.sb(ph, 'mx84', [P, 4, 8], F32)
                Rmx = Res('mx8')
                negm4 = self.sb(ph, 'negm4', [P, 12, P], F32)
                Rnegm4 = [Res() for i in range(3)]
                S.op('dve', lambda e: e.memset(negm4[:, :, :], 0.0), writes=Rnegm4)
                negb = [self.sb(ph, 'negb%d' % i, [P, T], BF16) for i in range(2)]
                Rnegb = [Res() for i in range(2)]
            else:
                wf = self.sb(ph, 'wf', [P, DC, P], BF16)
                Rwf = Res('wf')
                S.op('dve', lambda e: e.memset(wf[:, :, :], 0.0), writes=[Rwf])
                wfi = self.inp('fox_wf', [D, 8])
                S.aop('pool', lambda e: e.dma_start(out=wf[:, :, 0:8], in_=wfi.ap().rearrange("(c p) n -> p c n", p=P)),
                      reads=[], joins=[Rwf])
                nbf = self.sb(ph, 'nbf', [P, 1], F32)
                Rnbf = Res('nbf')
                fo = VOFF['fbf']
                S.op('dve', lambda e: e.tensor_scalar(out=nbf[:, :], in0=vec[:, fo:fo + 1], scalar1=-1.0, scalar2=None, op0=ALU.mult),
                     reads=[self.Rvec], writes=[Rnbf])
                onesf = self.sb(ph, 'onesf', [P, T], F32)
                Ronesf = Res('onesf')
                S.op('dve', lambda e: e.memset(onesf[:, :], 1.0), writes=[Ronesf])
                cumt = [self.sb(ph, 'cumt%d' % i, [P, T], F32) for i in range(2)]
                Rcum = [Res() for i in range(2)]
                c3b = [self.sb(ph, 'c3b%d' % i, [P, T], BF16) for i in range(3)]
                Rc3b = [Res() for i in range(3)]
                r1 = self.sb(ph, 'r1', [P, T], F32)
                Rr1 = Res('r1')

            def head_A(hbuf, Rh_, slab_i, colblk):
                it = head_A.it
                head_A.it += 1
                ps, Rps = next_mm()
                w = wres[slab_i]

                def gmm(e):
                    for kc in range(DC):
                        ins = e.matmul(ps[:, :], lhsT=w[:, kc // 8, (kc % 8) * 512 + colblk * P:(kc % 8) * 512 + (colblk + 1) * P],
                                       rhs=hbuf[:, kc, :], start=(kc == 0), stop=(kc == DC - 1))
                    return ins
                S.op('pe', gmm, reads=[Rw, Rh_], writes=[Rps])
                sqb, Rsqb = tm['sqb%d' % (it % 2)], Rt['sqb%d' % (it % 2)]
                raw, Rraw = tm['raw%d' % (it % 2)], Rt['raw%d' % (it % 2)]
                S.op('act', lambda e: e.activation(out=sqb[:, :], in_=ps[:, :], func=AF.Square), reads=[Rps], writes=[Rsqb])
                S.op('dve', lambda e: e.tensor_copy(out=raw[:, :], in_=ps[:, :]), reads=[Rps], writes=[Rraw])
                return it

            def head_B(it, g_off, scale, dst_dram_rows, g):
                sqb, Rsqb = tm['sqb%d' % (it % 2)], Rt['sqb%d' % (it % 2)]
                raw, Rraw = tm['raw%d' % (it % 2)], Rt['raw%d' % (it % 2)]
                st, Rs_ = stb[it % 2], Rst[it % 2]
                S.op('pe', lambda e: e.matmul(st[:, :], lhsT=ones[:], rhs=sqb[:, :], start=True, stop=True),
                     reads=[Rsqb, self.Rcst], writes=[Rs_])
                rs, Rrs = tm['rs%d' % (it % 2)], Rt['rs%d' % (it % 2)]
                S.op('act', lambda e: e.activation(out=rs[:, :], in_=st[:, :], func=AF.Ln, scale=1.0 / P, bias=self.eps_ap),
                     reads=[Rs_, self.Rcst], writes=[Rrs])
                rstd, Rrstd = tm['rstd%d' % (it % 2)], Rt['rstd%d' % (it % 2)]
                S.op('act', lambda e: e.activation(out=rstd[:, :], in_=rs[:, :], func=AF.Exp, scale=-0.5), reads=[Rrs], writes=[Rrstd])
                kn, Rkn = tm['kn%d' % (it % 2)], Rt['kn%d' % (it % 2)]
                S.op('dve', lambda e: e.scalar_tensor_tensor(out=kn[:, :], in0=raw[:, :], scalar=vec[:, g_off:g_off + 1], in1=rstd[:, :],
                                                             op0=ALU.mult, op1=ALU.mult), reads=[Rraw, Rrstd, self.Rvec], writes=[Rkn])
                ob, Rob = tm['ob%d' % (it % 2)], Rt['ob%d' % (it % 2)]
                S.op('act', lambda e: e.activation(out=ob[:, :], in_=kn[:, :], func=AF.Identity, scale=scale), reads=[Rkn], writes=[Rob])
                S.aop('sp', lambda e: e.dma_start(out=dst_dram_rows[:, g * T:(g + 1) * T], in_=ob[:, :]), reads=[Rob])
                return kn, Rkn
            head_A.it = 0
            deferred = []

            def load_h(g):
                hbuf, Rh_ = hb[g % nhb], Rhb[g % nhb]
                rank, i = g // NT, g % NT
                S.aop('sp', lambda e: e.dma_start(
                    out=hbuf[:, :, :], in_=self.hfull[i].ap()[rank * D:(rank + 1) * D, :].rearrange("(c p) t -> p c t", p=P)),
                    reads=[self.Rhfull[i]], writes=[Rh_])
            load_h(0)
            for g in range(2 * NT):
                hbuf, Rh_ = hb[g % nhb], Rhb[g % nhb]
                if nhb == 2 and g + 1 < 2 * NT:
                    load_h(g + 1)
                elif nhb == 1 and g > 0:
                    load_h(g)
                jobs = [('k', hd) for hd in range(NH)] + [('q', hd) for hd in range(NH)]

                def job_A(job):
                    kq, hd = job
                    return head_A(hbuf, Rh_, (2 if kq == 'k' else 0) + hd // 4, hd % 4)
                it_cur = job_A(jobs[0])
                for ji, (kq, hd) in enumerate(jobs):
                    it_next = job_A(jobs[ji + 1]) if ji + 1 < len(jobs) else None
                    if kq == 'k':
                        kn, Rkn = head_B(it_cur, kg_o, 1.0, self.KT.ap()[hd * P:(hd + 1) * P, :], g)
                        it_cur = it_next
                        if moba and 'pj_nored' not in self.opts:
                            S.op('dve', lambda e, kn=kn, hd=hd, g=g: e.tensor_reduce(
                                out=kmean[:, hd, 2 * g:2 * g + 2], in_=kn[:, :].rearrange("p (a b) -> p a b", b=256), axis=AX.X, op=ALU.add),
                                reads=[Rkn], joins=[Rkm])
                        continue
                    qn, Rqn = head_B(it_cur, qg_o, SCALE, self.QT.ap()[hd * P:(hd + 1) * P, :], g)
                    it_cur = it_next
                    if not moba or 'pj_nogate' in self.opts:
                        continue
                    nb_i = (g * NH + hd) % 2
                    hp = hd % 2

                    hp = hd % 3
                    curs = [2 * g + sub // 2 for sub in range(4)]
                    nm4 = negm4[:, hp * 4:(hp + 1) * 4, :]
                    Rnm = Rnegm4[hp]

                    def stage_a(hd=hd, g=g, qn=qn, Rqn=Rqn, curs=curs, nm4=nm4, Rnm=Rnm):
                        first_mm = True
                        for sub in range(4):
                            if curs[sub] > 3:
                                S.op('pe', lambda e, sub=sub: e.matmul(
                                    psx[:, sub * 32:(sub + 1) * 32], lhsT=qn[:, sub * P:(sub + 1) * P], rhs=kmean[:, hd, :], start=True, stop=True,
                                    skip_group_check=True),
                                    reads=[Rqn, Rkm], **({'writes': [Rpsx]} if first_mm else {'joins': [Rpsx]}))
                                first_mm = False
                        S.op('dve', lambda e: e.memset(nm4[:, :, 0:32], NEGM), writes=[Rnm])
                        if curs[3] > 3:
                            S.op('dve', lambda e: e.memset(gsb4[:, :, :], -1e30), writes=[Rgsb])
                        for sub in range(4):
                            cur = curs[sub]
                            if cur > 3:
                                S.op('dve', lambda e, cur=cur, sub=sub: e.tensor_copy(out=gsb4[:, sub, 0:cur], in_=psx[:, sub * 32:sub * 32 + cur]),
                                     reads=[Rpsx], joins=[Rgsb])
                                S.op('dve', lambda e, sub=sub: e.max(out=mx84[:, sub, :], in_=gsb4[:, sub, :]), reads=[Rgsb], joins=[Rmx])
                                S.op('dve', lambda e, cur=cur, sub=sub: e.tensor_scalar(
                                    out=nm4[:, sub, 0:cur], in0=gsb4[:, sub, 0:cur], scalar1=mx84[:, sub, 2:3], scalar2=NEGM, op0=ALU.is_lt, op1=ALU.mult),
                                    reads=[Rgsb, Rmx], joins=[Rnm])
                                S.op('dve', lambda e, cur=cur, sub=sub: e.memset(nm4[:, sub, cur:cur + 1], 0.0), joins=[Rnm])
                            else:
                                S.op('dve', lambda e, cur=cur, sub=sub: e.memset(nm4[:, sub, 0:cur + 1], 0.0), joins=[Rnm])

                    def stage_b(hd=hd, g=g, nb_i=nb_i, nm4=nm4, Rnm=Rnm):
                        for sub in range(4):
                            S.op('pe', lambda e, sub=sub: e.transpose(out=pstr[:, sub * P:(sub + 1) * P], in_=nm4[:, sub, :], identity=self.ident[:, :]),
                                 reads=[Rnm, self.Rcst], **({'writes': [Rpstr]} if sub == 0 else {'joins': [Rpstr]}))
                        nbuf, Rnb = negb[nb_i], Rnegb[nb_i]
                        S.op('act', lambda e: e.copy(out=nbuf[:, :], in_=pstr[:, :]), reads=[Rpstr], writes=[Rnb])
                        S.aop('sp', lambda e: e.dma_start(out=self.negT.ap()[hd * 32:(hd + 1) * 32, g * T:(g + 1) * T], in_=nbuf[0:32, :]),
                              reads=[Rnb])
                    if len(deferred) >= 2:
                        deferred.pop(0)()
                    stage_a()
                    deferred.append(stage_b)
                for fn_ in deferred:
                    fn_()
                deferred[:] = []
                vs0 = 4 if moba else 4
                for tt in range(0 if 'pj_nov' in self.opts else 4):
                    for vs in range(2):
                        ps, Rps = next_mm()
                        w = wres[vs0 + vs]

                        def gv(e, ps=ps, w=w, tt=tt, hbuf=hbuf):
                            for kc in range(DC):
                                ins = e.matmul(ps[:, :], lhsT=hbuf[:, kc, tt * P:(tt + 1) * P], rhs=w[:, kc // 8, (kc % 8) * 512:(kc % 8 + 1) * 512],
                                               start=(kc == 0), stop=(kc == DC - 1))
                            return ins
                        S.op('pe', gv, reads=[Rw, Rh_], writes=[Rps])
                        vi = (tt * 2 + vs) % 2
                        vb_, Rvb_ = tm['vb%d' % vi], Rt['vb%d' % vi]
                        S.op('act', lambda e, ps=ps, vb_=vb_: e.copy(out=vb_[:, :], in_=ps[:, :]), reads=[Rps], writes=[Rvb_])
                        r0 = g * T + tt * P
                        S.aop('sp', lambda e, vb_=vb_, r0=r0, vs=vs: e.dma_start(out=self.Vs.ap()[r0:r0 + P, vs * 512:(vs + 1) * 512], in_=vb_[:, :]),
                              reads=[Rvb_])
                if moba:
                    continue
                for ch in range(NH):
                    ps, Rps = next_mm()
                    w = wres[6 + ch // 4]
                    cb_ = ch % 4

                    def gg(e, ps=ps, w=w, cb_=cb_, hbuf=hbuf):
                        for kc in range(DC):
                            ins = e.matmul(ps[:, :], lhsT=w[:, kc // 8, (kc % 8) * 512 + cb_ * P:(kc % 8) * 512 + (cb_ + 1) * P],
                                           rhs=hbuf[:, kc, :], start=(kc == 0), stop=(kc == DC - 1))
                        return ins
                    S.op('pe', gg, reads=[Rw, Rh_], writes=[Rps])
                    ob, Rob = tm['ob%d' % (ch % 2)], Rt['ob%d' % (ch % 2)]
                    S.op('act', lambda e, ps=ps, ob=ob: e.activation(out=ob[:, :], in_=ps[:, :], func=AF.Sigmoid), reads=[Rps], writes=[Rob])
                    S.aop('sp', lambda e, ob=ob, ch=ch, g=g: e.dma_start(out=self.SGT.ap()[ch * P:(ch + 1) * P, g * T:(g + 1) * T], in_=ob[:, :]),
                          reads=[Rob])
                def gf(e, hbuf=hbuf):
                    for kc in range(DC):
                        ins = e.matmul(psx[:, :], lhsT=wf[:, kc, :], rhs=hbuf[:, kc, :], start=(kc == 0), stop=(kc == DC - 1))
                    return ins
                S.op('pe', gf, reads=[Rwf, Rh_], writes=[Rpsx])
                e1, Re1 = tm['raw0'], Rt['raw0']
                S.op('act', lambda e: e.activation(out=e1[:, :], in_=psx[:, :], func=AF.Exp, scale=-1.0, bias=nbf[:, 0:1]),
                     reads=[Rpsx, Rnbf], writes=[Re1])
                l1, Rl1 = tm['raw1'], Rt['raw1']
                S.op('act', lambda e: e.activation(out=l1[:, :], in_=e1[:, :], func=AF.Ln, bias=self.one_ap, scale=1.0),
                     reads=[Re1, self.Rcst], writes=[Rl1])
                cm, Rcm = cumt[g % 2], Rcum[g % 2]
                pv, Rpv = cumt[(g + 1) % 2], Rcum[(g + 1) % 2]
                init = 0.0 if g == 0 else pv[:, T - 1:T]
                S.op('dve', lambda e, cm=cm, init=init: e.tensor_tensor_scan(out=cm[:, :], data0=onesf[:, :], data1=l1[:, :], initial=init,
                                                                            op0=ALU.mult, op1=ALU.subtract),
                     reads=[Rl1, Ronesf] + ([Rpv] if g else []), writes=[Rcm])
                S.op('act', lambda e, cm=cm: e.copy(out=c3b[0][:, :], in_=cm[:, :]), reads=[Rcm], writes=[Rc3b[0]])
                S.op('dve', lambda e, cm=cm: e.tensor_tensor(out=r1[:, :], in0=cm[:, :], in1=c3b[0][:, :], op=ALU.subtract),
                     reads=[Rcm, Rc3b[0]], writes=[Rr1])
                S.op('act', lambda e: e.copy(out=c3b[1][:, :], in_=r1[:, :]), reads=[Rr1], writes=[Rc3b[1]])
                S.op('dve', lambda e: e.tensor_tensor(out=r1[:, :], in0=r1[:, :], in1=c3b[1][:, :], op=ALU.subtract),
                     reads=[Rc3b[1], Rr1], writes=[Rr1])
                S.op('act', lambda e: e.copy(out=c3b[2][:, :], in_=r1[:, :]), reads=[Rr1], writes=[Rc3b[2]])
                for k3 in range(3):
                    S.aop('sp', lambda e, k3=k3, g=g: e.dma_start(out=self.c3.ap()[k3 * 8:(k3 + 1) * 8, g * T:(g + 1) * T], in_=c3b[k3][0:8, :]),
                          reads=[Rc3b[k3]])
                for tt in range(4):
                    S.op('pe', lambda e, cm=cm, tt=tt: e.transpose(out=pstr[:, tt * P:(tt + 1) * P], in_=cm[:, tt * P:(tt + 1) * P], identity=self.ident[:, :]),
                         reads=[Rcm, self.Rcst], **({'writes': [Rpstr]} if tt == 0 else {'joins': [Rpstr]}))
                S.op('dve', lambda e, g=g: e.tensor_scalar(out=self.negcumS[:, g * 4:(g + 1) * 4, :],
                                                           in0=pstr[:, :].rearrange("p (a b) -> p a b", b=P)[:, :, 0:NH],
                                                           scalar1=-1.0, scalar2=None, op0=ALU.mult),
                     reads=[Rpstr], joins=[self.Rncs])
            S.flush('proj_' + kind)

    def phase_attn(self, kind):
        nc, S = self.nc, self.S
        moba = (kind == 'moba')
        with ExitStack() as ph:
            cin = self.inp('cst', [P, NCST])
            ne = 32 if moba else NH
            eoff = CST_E if moba else CST_EF
            etmp = self.sb(ph, 'etmp', [P, ne * P], F32)
            Retmp = Res('etmp')
            S.aop('sp', lambda e: e.dma_start(out=etmp[:, :], in_=cin.ap()[:, eoff:eoff + ne * P]), writes=[Retmp])
            Eb = self.sb(ph, 'Eb', [P, ne, P], BF16)
            REb = Res('Eb')
            S.op('dve', lambda e: e.tensor_copy(out=Eb[:, :, :].rearrange("p a b -> p (a b)"), in_=etmp[:, :]), reads=[Retmp], writes=[REb])
            qt = [self.sb(ph, 'qt%d' % i, [P, SEQ], BF16) for i in range(2)]
            kt = [self.sb(ph, 'kt%d' % i, [P, SEQ], BF16) for i in range(2)]
            vh = [self.sb(ph, 'vh%d' % i, [P, 64, P], BF16) for i in range(2)]
            Rq = [Res() for i in range(2)]
            Rk = [Res() for i in range(2)]
            Rv = [Res() for i in range(2)]
            if moba:
                bias_t = [self.sb(ph, 'ngt%d' % i, [P, SEQ], BF16) for i in range(2)]
                Rbias = [Res() for i in range(2)]
                for i in range(2):
                    S.op('pool', lambda e, i=i: e.memset(bias_t[i][:, :], 0.0), writes=[Rbias[i]])
            else:
                c3t = self.sb(ph, 'c3t', [P, SEQ], BF16)
                Rc3t = Res('c3t')
                S.op('pool', lambda e: e.memset(c3t[:, :], 0.0), writes=[Rc3t])
                S.aop('sp', lambda e: e.dma_start(out=c3t[0:24, :], in_=self.c3.ap()), reads=[], joins=[Rc3t])
                sgh = [self.sb(ph, 'sgh%d' % i, [P, SEQ], BF16) for i in range(2)]
                Rsg = [Res() for i in range(2)]
            pT = [self.sb(ph, 'pT%d' % i, [P, T], BF16) for i in range(3)]
            RpT = [Res() for i in range(3)]
            oh = [self.sb(ph, 'oh%d' % i, [P, T], BF16) for i in range(2)]
            Roh = [Res() for i in range(2)]
            rden = self.sb(ph, 'rden', [P, T], F32)
            Rrden = Res('rden')
            otmp = self.sb(ph, 'otmp', [P, T], F32)
            Rotmp = Res('otmp')
            pss = [self.ps(ph, 'pss%d' % i) for i in range(3)]
            Rpss = [PRes() for i in range(3)]
            oacc = [self.ps(ph, 'oacc%d' % i) for i in range(2)]
            Roacc = [PRes() for i in range(2)]
            dacc = [self.ps(ph, 'dacc%d' % i) for i in range(2)]
            Rdacc = [PRes() for i in range(2)]
            ones, tri = self.ones_bf, self.tri_bf
            def load_head(hd):
                b2 = hd % 2
                S.aop('sp', lambda e: e.dma_start(out=qt[b2][:, :], in_=self.QT.ap()[hd * P:(hd + 1) * P, :]), writes=[Rq[b2]])
                S.aop('sp', lambda e: e.dma_start(out=kt[b2][:, :], in_=self.KT.ap()[hd * P:(hd + 1) * P, :]), writes=[Rk[b2]])
                for part in range(4):
                    S.aop('sp', lambda e, part=part: e.dma_start(
                        out=vh[b2][:, part * 16:(part + 1) * 16, :],
                        in_=self.Vs.ap()[part * 2048:(part + 1) * 2048, hd * P:(hd + 1) * P].rearrange("(c p) d -> p c d", p=P)),
                        **({'writes': [Rv[b2]]} if part == 0 else {'joins': [Rv[b2]]}))
                if moba:
                    S.aop('sp', lambda e: e.dma_start(out=bias_t[b2][0:32, :], in_=self.negT.ap()[hd * 32:(hd + 1) * 32, :]),
                          reads=[], joins=[Rbias[b2]])
                else:
                    S.aop('sp', lambda e: e.dma_start(out=sgh[b2][:, :], in_=self.SGT.ap()[hd * P:(hd + 1) * P, :]), writes=[Rsg[b2]])

            tiles = []
            for hd in range(NH):
                for j in range(2 * NT):
                    nsc = 4 * j + 4
                    for sc in range(nsc):
                        tiles.append((hd, j, sc, nsc))

            def emit_scores(i):
                hd, j, sc, nsc = tiles[i]
                b2 = hd % 2
                q_, k_ = qt[b2], kt[b2]
                brhs = bias_t[b2] if moba else c3t
                Rb_ = Rbias[b2] if moba else Rc3t
                r = sc - 4 * j if sc >= 4 * j else 0
                c0 = r * P
                n = T - c0
                q0 = j * T + c0
                ps, Rps = pss[i % 3], Rpss[i % 3]
                pt, Rpt = pT[i % 3], RpT[i % 3]
                esel = (sc // 2) if moba else hd
                diag = sc >= 4 * j

                def gs(e):
                    e.matmul(ps[:, c0:c0 + n], lhsT=k_[:, sc * P:(sc + 1) * P], rhs=q_[:, q0:q0 + n], start=True, stop=False)
                    ins = e.matmul(ps[:, c0:c0 + n], lhsT=Eb[:, esel, :], rhs=brhs[:, q0:q0 + n], start=False, stop=not diag,
                                   skip_group_check=True)
                    if diag:
                        ins = e.matmul(ps[:, c0:c0 + P], lhsT=self.ident_bf[:, :], rhs=self.trineg[:, :], start=False, stop=True,
                                       skip_group_check=True)
                    return ins
                S.op('pe', gs, reads=[Rk[b2], Rq[b2], REb, Rb_, self.Rcst], writes=[Rps])
                if moba:
                    S.op('act', lambda e: e.activation(out=pt[:, 0:n], in_=ps[:, c0:c0 + n], func=AF.Exp), reads=[Rps], writes=[Rpt])
                else:
                    S.op('act', lambda e: e.activation(out=pt[:, 0:n], in_=ps[:, c0:c0 + n], func=AF.Exp,
                                                       bias=self.negcumS[:, sc, hd:hd + 1], scale=1.0),
                         reads=[Rps, self.Rncs], writes=[Rpt])

            def emit_pv(i):
                hd, j, sc, nsc = tiles[i]
                b2 = hd % 2
                v_ = vh[b2]
                r = sc - 4 * j if sc >= 4 * j else 0
                c0 = r * P
                n = T - c0
                pt, Rpt = pT[i % 3], RpT[i % 3]
                oa, Roa = oacc[j % 2], Roacc[j % 2]
                da, Rda = dacc[j % 2], Rdacc[j % 2]
                first, last = (sc == 0), (sc == nsc - 1)

                def gpv(e):
                    e.matmul(oa[:, c0:c0 + n], lhsT=v_[:, sc, :], rhs=pt[:, 0:n], start=first, stop=last, skip_group_check=True)
                    return e.matmul(da[:, c0:c0 + n], lhsT=ones[:, :], rhs=pt[:, 0:n], start=first, stop=last, skip_group_check=True)
                kw = {'writes': [Roa, Rda]} if first else {'joins': [Roa, Rda]}
                S.op('pe', gpv, reads=[Rv[b2], Rpt, self.Rcst], **kw)
                if not last:
                    return
                S.op('dve', lambda e: e.reciprocal(out=rden[:, :], in_=da[:, :]), reads=[Rda], writes=[Rrden])
                o_, Ro_ = oh[j % 2], Roh[j % 2]
                if moba:
                    S.op('dve', lambda e: e.tensor_tensor(out=o_[:, :], in0=oa[:, :], in1=rden[:, :], op=ALU.mult),
                         reads=[Roa, Rrden], writes=[Ro_])
                else:
                    S.op('dve', lambda e: e.tensor_tensor(out=otmp[:, :], in0=oa[:, :], in1=rden[:, :], op=ALU.mult),
                         reads=[Roa, Rrden], writes=[Rotmp])
                    S.op('dve', lambda e: e.tensor_tensor(out=o_[:, :], in0=otmp[:, :], in1=sgh[b2][:, j * T:(j + 1) * T], op=ALU.mult),
                         reads=[Rotmp, Rsg[b2]], writes=[Ro_])
                S.aop('sp', lambda e: e.dma_start(out=self.osend[hd].ap()[:, j * T:(j + 1) * T], in_=o_[:, :]),
                      reads=[Ro_], joins=[self.Rosend[hd]])
                if j == 2 * NT - 1:
                    self.allgather(self.osend[hd], self.ofull[hd], self.Rosend[hd], self.Rofull[hd])

            load_head(0)
            load_head(1)
            emit_scores(0)
            for i in range(len(tiles)):
                hd, j, sc, nsc = tiles[i]
                if j == 0 and sc == 0 and 1 <= hd < NH - 1:
                    load_head(hd + 1)
                if i + 1 < len(tiles):
                    emit_scores(i + 1)
                emit_pv(i)
            S.flush('attn_' + kind)

    def phase_tail(self, l, wo_name, x_dst, Rdst, emit_next):
        S = self.S
        cf = self.coef
        fo = VOFF['flag']
        with ExitStack() as ph:
            self.alloc_tok(ph, nslab=4)
            xt, Rxt, h, Rh, u, Ru = self.xt, self.Rxt, self.h, self.Rh, self.u, self.Ru
            cand = [u[:, 0:16, :], u[:, 16:32, :]]
            for i in range(NT):
                S.aop('sp', lambda e, i=i: e.dma_start(out=xt[:, :, :], in_=self.tile_io(self.xres.ap(), i)), reads=[self.Rxres], writes=[Rxt])
                for half in range(2):
                    for hd in range(NH):
                        c0 = half * TOK + i * T
                        S.aop('sp', lambda e, half=half, hd=hd, c0=c0: e.dma_start(
                            out=u[:, half * 16 + hd:half * 16 + hd + 9:8, :],
                            in_=self.ofull[hd].ap().rearrange("(r p) t -> p r t", r=2)[:, :, c0:c0 + T]),
                            reads=[self.Rofull[hd]], **({'writes': [Ru]} if (half == 0 and hd == 0) else {'joins': [Ru]}))
                S.op('dve', lambda e: e.tensor_scalar(out=cand[0], in0=cand[0], scalar1=self.nflag[:, 0:1], scalar2=None, op0=ALU.mult),
                     reads=[Ru, self.Rcoef], joins=[Ru])
                S.op('dve', lambda e: e.scalar_tensor_tensor(out=h[:, :, :], in0=cand[1], scalar=self.vec[:, fo:fo + 1], in1=cand[0],
                                                             op0=ALU.mult, op1=ALU.add), reads=[Ru, self.Rvec], writes=[Rh])
                for g in range(4):
                    slab, Rs = self.load_slab(wo_name, g)
                    for mm in range(4):
                        m = g * 4 + mm
                        ps, Rps = self.next_mm()
                        self.mm_group(ps[:, :], Rps, slab, Rs, range(DC),
                                      lambda kc, mm=mm, slab=slab: slab[:, kc // 8, (kc % 8) * 512 + mm * P:(kc % 8) * 512 + (mm + 1) * P],
                                      lambda kc: h[:, kc, :], [Rh])
                        S.op('dve', lambda e, ps=ps, m=m: e.scalar_tensor_tensor(
                            out=xt[:, m, :], in0=ps[:, :], scalar=cf[:, l, 2, m:m + 1], in1=xt[:, m, :], op0=ALU.mult, op1=ALU.add),
                            reads=[Rps, self.Rcoef, Rxt], joins=[Rxt])
                self.mlp(l, xt, Rxt)
                S.aop('sp', lambda e, i=i: e.dma_start(out=self.tile_io(x_dst, i), in_=xt[:, :, :]), reads=[Rxt], joins=[Rdst])
                if emit_next is not None:
                    emit_next(i, xt, Rxt)
            S.flush('tail%d' % l)

    def emit_halo(self):
        def f(i, xt, Rxt):
            if i != NT - 1:
                return
            S = self.S
            S.aop('sp', lambda e: e.dma_start(out=self.halosend.ap().rearrange("(c p) t -> p c t", p=P), in_=xt[:, :, T - HALO:T]),
                  reads=[Rxt], writes=[self.Rhalosend])
            self.allgather(self.halosend, self.halofull, self.Rhalosend, self.Rhalofull)
        return f

    def build(self):
        nc = self.nc
        with ExitStack() as top:
            self.S = S = Sched(nc, top)
            self.vec = self.sb(top, 'vec', [P, NV], F32)
            self.Rvec = Res('vec')
            self.ident = self.sb(top, 'ident', [P, P], F32)
            self.tri_bf = self.sb(top, 'tri', [P, P], BF16)
            self.ones_bf = self.sb(top, 'ones', [P, P], BF16)
            self.ident_bf = self.sb(top, 'identb', [P, P], BF16)
            self.trineg = self.sb(top, 'trineg', [P, P], BF16)
            self.epst = self.sb(top, 'epst', [P, 1], F32)
            self.eps_ap = self.epst[:, 0:1]
            self.Rcst = Res('cst')
            self.coef = self.sb(top, 'coef', [P, 4, 6, 16], F32)
            self.cb = self.sb(top, 'cb', [P, 2, 16], F32)
            self.Rcoef = Res('coef')
            self.setup_weights()
            self.xres = self.dram('xres', [D, TOK], F32)
            self.Rxres = Res('xres')
            self.hsend = [self.dram('hsend%d' % i, [D, T], BF16) for i in range(NT)]
            self.hfull = [self.dram('hfull%d' % i, [2 * D, T], BF16) for i in range(NT)]
            self.Rhsend = [Res() for i in range(NT)]
            self.Rhfull = [Res() for i in range(NT)]
            xin = self.inp('xT', [D, TOK])
            halo0 = self.inp('halo0', [D, HALO])
            Rin = Res('xin')
            out = nc.dram_tensor('outT', [D, TOK], F32, kind="ExternalOutput")
            Rout = Res('out')

            self.one_t = self.sb(top, 'onet', [P, 1], F32)
            self.one_ap = self.one_t[:, 0:1]
            self.nflag = self.sb(top, 'nflag', [P, 1], F32)
            self.negcumS = self.sb(top, 'ncs', [P, 64, NH], F32)
            self.Rncs = Res('ncs')
            self.QT = self.dram('QT', [NH * P, SEQ], BF16)
            self.KT = self.dram('KT', [NH * P, SEQ], BF16)
            self.Vs = self.dram('Vs', [SEQ, NH * P], BF16)
            self.SGT = self.dram('SGT', [NH * P, SEQ], BF16)
            self.negT = self.dram('negT', [NH * 32, SEQ], BF16)
            self.c3 = self.dram('c3', [24, SEQ], BF16)
            self.osend = [self.dram('osend%d' % i, [P, SEQ], BF16) for i in range(NH)]
            self.ofull = [self.dram('ofull%d' % i, [2 * P, SEQ], BF16) for i in range(NH)]
            self.Rosend = [Res() for i in range(NH)]
            self.Rofull = [Res() for i in range(NH)]
            for nm in self.dump_names:
                self.dumps.append((nm, getattr(self, nm)))
            self.halosend = self.dram('halosend', [D, HALO], F32)
            self.halofull = self.dram('halofull', [2 * D, HALO], F32)
            self.Rhalosend, self.Rhalofull = Res(), Res()

            stop = min(self.stop_after, 7)
            final_step = {0: 0, 1: 0, 2: 0, 3: 3, 4: 3, 5: 3, 6: 6, 7: 7}[stop]

            def dst(step):
                return (out.ap(), Rout) if step == final_step else (self.xres.ap(), self.Rxres)
            self.phase_prep()
            if stop >= 1:
                self.prep_local()
            if stop >= 3:
                for bi in (3, 4, 5):
                    self.prep_bundle(bi)
            d_, R_ = dst(0)
            self.phase_conv_layer(0, 0, xin.ap(), Rin, halo0.ap().rearrange("(c p) t -> p c t", p=P), Rin,
                                  d_, R_, None if stop == 0 else self.emit_hsend(1))
            if stop >= 1:
                if stop >= 6:
                    for bi in (6, 7, 8):
                        self.prep_bundle(bi)
                if 'skip_proj' not in self.opts:
                    self.phase_proj('moba')
            if stop >= 2:
                self.phase_attn('moba')
            if stop >= 3:
                d_, R_ = dst(3)
                self.phase_tail(1, 'mwo', d_, R_, None if stop == 3 else self.emit_hsend(2))
            if stop >= 4:
                if stop >= 7:
                    for bi in (9, 10, 11):
                        self.prep_bundle(bi)
                self.phase_proj('fox')
            if stop >= 5:
                self.phase_attn('fox')
            if stop >= 6:
                d_, R_ = dst(6)
                self.phase_tail(2, 'fwo', d_, R_, None if stop == 6 else self.emit_halo())
            if stop >= 7:
                self.phase_conv_layer(3, 1, self.xres.ap(), self.Rxres,
                                      self.halofull.ap()[0:D, :].rearrange("(c p) t -> p c t", p=P), self.Rhalofull,
                                      out.ap(), Rout, None)
            for nm, t in self.dumps:
                o_ = nc.dram_tensor('dump_' + nm, list(t.shape), t.dtype, kind="ExternalOutput")
                S.aop('sp', lambda e, o_=o_, t=t: e.dma_start(out=o_.ap(), in_=t.ap()))
            S.flush('end', drain=('dma', 'bg', 'cc'))
        return nc


def _pvec(v):
    return np.ascontiguousarray(np.asarray(v, np.float32).reshape(-1, P).T)


def make_cst():
    cst = np.zeros((P, NCST), np.float32)
    cst[:, CST_IDENT:CST_IDENT + P] = np.eye(P, dtype=np.float32)
    cst[:, CST_TRI:CST_TRI + P] = np.triu(np.ones((P, P), np.float32))
    cst[:, CST_ONES:CST_ONES + P] = 1.0
    for n in range(32):
        cst[n, CST_E + n * P:CST_E + (n + 1) * P] = 1.0
    for hd in range(8):
        for r in (hd, 8 + hd, 16 + hd):
            cst[r, CST_EF + hd * P:CST_EF + (hd + 1) * P] = 1.0
    return cst


def make_vecs(inp, b, hf):
    vec = np.zeros((P, NV), np.float32)

    def put(name, arr):
        arr = np.asarray(arr, np.float32)
        if arr.ndim == 1:
            arr = arr[:, None]
        vec[:arr.shape[0], VOFF[name]:VOFF[name] + arr.shape[1]] = arr
    put('c', _pvec(inp['c'][b]))
    for l in range(4):
        put('n1g%d' % l, _pvec(inp['norm1_g'][l]))
        put('n2g%d' % l, _pvec(inp['norm2_g'][l]))
    mb = np.zeros((P, 192), np.float32)
    for l in range(4):
        for jj in range(3):
            j = hf * 3 + jj
            mb[:, l * 48 + jj * 16:l * 48 + jj * 16 + 16] = _pvec(inp['mod_b'][l, j * D:(j + 1) * D])
    put('modb', mb)
    for cj in range(2):
        put('cbia%d' % cj, _pvec(inp['conv_b_in'][cj, :D]))
        put('cbig%d' % cj, _pvec(inp['conv_b_in'][cj, D:]))
        w = np.asarray(inp['conv_dw_w'][cj], np.float32)
        put('cdww%d' % cj, np.ascontiguousarray(w.reshape(CW, 16, P).transpose(2, 1, 0)).reshape(P, 16 * CW))
        put('cdwb%d' % cj, _pvec(inp['conv_dw_b'][cj]))
        put('clng%d' % cj, _pvec(inp['conv_ln_g'][cj]))
        put('clnb%d' % cj, _pvec(inp['conv_ln_b'][cj]))
        put('cbo%d' % cj, _pvec(inp['conv_b_out'][cj]))
    put('mqg', inp['moba_q_g'][0])
    put('mkg', inp['moba_k_g'][0])
    put('fqg', inp['fox_q_g'][0])
    put('fkg', inp['fox_k_g'][0])
    put('fbf', inp['fox_b_f'][0][hf * 8:(hf + 1) * 8])
    put('flag', np.full((P,), float(hf), np.float32))
    return vec


def core_inputs(inp, names, b, hf, cst):
    kh = slice(hf * 1024, (hf + 1) * 1024)
    hs = slice(hf * 1024, (hf + 1) * 1024)
    out = {}
    for nm in names:
        if nm == 'xT':
            out[nm] = np.ascontiguousarray(inp['x'][b, hf * TOK:(hf + 1) * TOK, :].T)
        elif nm == 'halo0':
            out[nm] = np.ascontiguousarray(inp['x'][b, TOK - HALO:TOK, :].T)
        elif nm == 'vecs':
            out[nm] = make_vecs(inp, b, hf)
        elif nm == 'cst':
            out[nm] = cst
        elif nm == 'mod_w_h':
            out[nm] = np.ascontiguousarray(inp['mod_w'][:, :, hf * 6144:(hf + 1) * 6144])
        elif nm.startswith('mlp_w1h_'):
            out[nm] = np.ascontiguousarray(inp['mlp_w1'][int(nm[-1]), kh, :])
        elif nm.startswith('mlp_w2h_'):
            out[nm] = np.ascontiguousarray(inp['mlp_w2'][int(nm[-1]), hf * 4096:(hf + 1) * 4096, :])
        elif nm.startswith('conv_w_in_h_'):
            out[nm] = np.ascontiguousarray(inp['conv_w_in'][int(nm[-1]), kh, :])
        elif nm.startswith('conv_w_out_h_'):
            out[nm] = np.ascontiguousarray(inp['conv_w_out'][int(nm[-1]), kh, :])
        elif nm == 'moba_wo_h':
            out[nm] = np.ascontiguousarray(inp['moba_w_o'][0, kh, :])
        elif nm == 'fox_wo_h':
            out[nm] = np.ascontiguousarray(inp['fox_w_o'][0, kh, :])
        elif nm == 'moba_qkv_l':
            w = inp['moba_w_qkv'][0]
            out[nm] = np.ascontiguousarray(np.concatenate([w[:, i * D:(i + 1) * D][:, hs] for i in range(3)], axis=1))
        elif nm == 'fox_w4_l':
            w = inp['fox_w_in'][0]
            out[nm] = np.ascontiguousarray(np.concatenate([w[:, i * D:(i + 1) * D][:, hs] for i in range(4)], axis=1))
        elif nm == 'fox_wf':
            out[nm] = np.ascontiguousarray(inp['fox_w_in'][0][:, 4 * D + hf * 8:4 * D + (hf + 1) * 8])
        else:
            raise KeyError(nm)
    return out


_PROG = {}


def run_cores(inp, cores, stop_after=99, dumps=(), raw=False, opts=()):
    key = (stop_after, len(cores), tuple(dumps), tuple(opts))
    if key not in _PROG:
        pairs = [[2 * i, 2 * i + 1] for i in range(len(cores) // 2)]
        bld = Builder(stop_after=stop_after, pairs=pairs, dumps=dumps, opts=opts)
        nc = bld.build()
        _PROG[key] = (nc, list(bld.inputs.keys()))
    nc, names = _PROG[key]
    cst = make_cst()
    in_maps = [core_inputs(inp, names, r // 2, r % 2, cst) for r in cores]
    res = run_bass_kernel_spmd(nc, in_maps, core_ids=list(range(len(cores))))
    if raw:
        return res.results
    return [r['outT'] for r in res.results]


def kernel(**inputs):
    inp = {k: np.asarray(v) for k, v in inputs.items()}
    outs = run_cores(inp, list(range(8)))
    B = inp['x'].shape[0]
    y = np.empty((B, SEQ, D), np.float32)
    for r, o in enumerate(outs):
        b, hf = r // 2, r % 2
        y[b, hf * TOK:(hf + 1) * TOK, :] = o.T
    return y
```
